# Optimizing a Trainium2 kernel written in Bass

```python
import jax
import jax.numpy as jnp
from jax import lax
import numpy as np

D_MODEL = 1024
BATCH = 2
SEQ = 8192
DEPTH = 2

D_MIX = D_MODEL
D_GROUP = D_MIX // 4
HEAD_DIM = 64
N_HEADS = D_GROUP // HEAD_DIM
Q_BLOCK = 128
RWKV_LORA = 32
CONV_WIDTH = 4
LRU_C = 8.0
RMS_EPS = 1e-6
GN_EPS = 64e-5
N_RWKV_SHIFT = 3 * D_GROUP + 2 * RWKV_LORA
N_IN = 14 * D_GROUP + N_HEADS + 2 * RWKV_LORA

kernel_name = "hybrid_fox_stickbreak_rwkv7_rglru"


def rms_norm(x, g):
    x32 = x.astype(jnp.float32)
    y = x32 * lax.rsqrt(jnp.mean(x32 * x32, axis=-1, keepdims=True) + RMS_EPS)
    return (y * g.astype(jnp.float32)).astype(x.dtype)


def split_columns(p):
    G, H = D_GROUP, N_HEADS
    sizes = [G, G, G, G, H, G, G, G, G, N_RWKV_SHIFT, G, G, G]
    cuts, acc = [], 0
    for n in sizes[:-1]:
        acc += n
        cuts.append(acc)
    return jnp.split(p, cuts, axis=-1)


def to_heads(t):
    b, s, _ = t.shape
    return t.reshape(b, s, N_HEADS, HEAD_DIM).transpose(0, 2, 1, 3).astype(jnp.float32)


def query_blocks(t):
    b, h, s = t.shape[:3]
    t = t.reshape((b, h, s // Q_BLOCK, Q_BLOCK) + t.shape[3:])
    return jnp.moveaxis(t, 2, 0)


def merge_blocks(o):
    nb, b, h, q, d = o.shape
    return o.transpose(1, 0, 3, 2, 4).reshape(b, nb * q, h * d)


def forgetting_attention(q, k, v, log_f):
    s = q.shape[2]
    scale = HEAD_DIM ** -0.5
    cum = jnp.cumsum(log_f, axis=-1)
    key_pos = jnp.arange(s)

    def block(args):
        qb, cb, i = args
        q_pos = i * Q_BLOCK + jnp.arange(Q_BLOCK)
        logits = (jnp.einsum('bhqd,bhkd->bhqk', qb, k) * scale
                  + cb[..., None] - cum[:, :, None, :])
        logits = jnp.where(key_pos[None, :] <= q_pos[:, None], logits, -jnp.inf)
        return jnp.einsum('bhqk,bhkd->bhqd', jax.nn.softmax(logits, axis=-1), v)

    out = lax.map(block, (query_blocks(q), query_blocks(cum), jnp.arange(s // Q_BLOCK)))
    return merge_blocks(out)


def stick_breaking_attention(q, k, v):
    s = q.shape[2]
    scale = HEAD_DIM ** -0.5
    key_pos = jnp.arange(s)

    def block(args):
        qb, i = args
        q_pos = i * Q_BLOCK + jnp.arange(Q_BLOCK)
        z = jnp.einsum('bhqd,bhkd->bhqk', qb, k) * scale
        mask = key_pos[None, :] < q_pos[:, None]
        log_keep = jnp.where(mask, jax.nn.log_sigmoid(-z), 0.0)
        log_rest = lax.cumsum(log_keep, axis=3, reverse=True) - log_keep
        att = jnp.where(mask, jnp.exp(jax.nn.log_sigmoid(z) + log_rest), 0.0)
        return jnp.einsum('bhqk,bhkd->bhqd', att, v)

    out = lax.map(block, (query_blocks(q), jnp.arange(s // Q_BLOCK)))
    return merge_blocks(out)


def rwkv7_time_mix(p, mu, w0, w2, a0, a2, k_k, k_a, r_k, ln_g, ln_b):
    b, s, _ = p.shape
    G, R = D_GROUP, RWKV_LORA
    p = p.astype(jnp.float32)
    prev = jnp.pad(p, ((0, 0), (1, 0), (0, 0)))[:, :-1]
    p = p + (prev - p) * mu
    r, k, v, wl, al = jnp.split(p, [G, 2 * G, 3 * G, 3 * G + R], axis=-1)
    w = -jax.nn.softplus(-(w0 + jnp.tanh(wl) @ w2)) - 0.5
    decay = jnp.exp(-jnp.exp(w))
    a = jax.nn.sigmoid(a0 + al @ a2)
    hs = lambda t: t.reshape(b, s, N_HEADS, HEAD_DIM)
    kk = hs(k * k_k)
    kk = kk * lax.rsqrt(jnp.maximum(jnp.sum(kk * kk, axis=-1, keepdims=True), 1e-12))
    k = hs(k * (1.0 + (a - 1.0) * k_a))
    r, v, decay, a = hs(r), hs(v), hs(decay), hs(a)

    def step(state, inp):
        r_t, w_t, k_t, v_t, kk_t, a_t = inp
        sa = jnp.einsum('bhij,bhj->bhi', state, -kk_t)
        state = (state * w_t[:, :, None, :]
                 + sa[..., None] * (kk_t * a_t)[:, :, None, :]
                 + v_t[..., None] * k_t[:, :, None, :])
        return state, jnp.einsum('bhij,bhj->bhi', state, r_t)

    xs = tuple(jnp.moveaxis(t, 1, 0) for t in (r, decay, k, v, kk, a))
    state0 = jnp.zeros((b, N_HEADS, HEAD_DIM, HEAD_DIM), jnp.float32)
    _, y = lax.scan(step, state0, xs)
    y = jnp.moveaxis(y, 0, 1)
    mean = jnp.mean(y, axis=-1, keepdims=True)
    var = jnp.mean(jnp.square(y - mean), axis=-1, keepdims=True)
    y = ((y - mean) * lax.rsqrt(var + GN_EPS)).reshape(b, s, G) * ln_g + ln_b
    bonus = jnp.sum(r * k * r_k, axis=-1, keepdims=True) * v
    return y + bonus.reshape(b, s, G)


def rg_lru(x, conv_w, conv_b, w_a, b_a, w_x, b_x, lam):
    x = x.astype(jnp.float32)
    b, s, c = x.shape
    xc = lax.conv_general_dilated(
        x, conv_w.astype(jnp.float32)[:, None, :], window_strides=(1,),
        padding=[(CONV_WIDTH - 1, 0)], dimension_numbers=('NWC', 'WIO', 'NWC'),
        feature_group_count=c) + conv_b
    xh = xc.reshape(b, s, N_HEADS, HEAD_DIM)
    r = jax.nn.sigmoid(jnp.einsum('bsni,nij->bsnj', xh, w_a).reshape(b, s, c) + b_a)
    i = jax.nn.sigmoid(jnp.einsum('bsni,nij->bsnj', xh, w_x).reshape(b, s, c) + b_x)
    log_a = -LRU_C * r * jax.nn.softplus(-lam)
    a = jnp.exp(log_a)
    u = jnp.sqrt(-jnp.expm1(2.0 * log_a)) * (i * xc)

    def combine(lhs, rhs):
        a1, b1 = lhs
        a2, b2 = rhs
        return a1 * a2, a2 * b1 + b2

    _, h = lax.associative_scan(combine, (a, u), axis=1)
    return h


def setup_inputs(seed: int = 0) -> dict:
    key = jax.random.key(seed)
    ks = jax.random.split(key, 24)
    f32 = jnp.float32
    L, D, G, H, R, N = DEPTH, D_MODEL, D_GROUP, N_HEADS, RWKV_LORA, HEAD_DIM

    def nrm(k, shape, scale):
        return scale * jax.random.normal(k, shape, f32)

    x = jax.random.normal(ks[0], (BATCH, SEQ, D), f32)
    norm_g = 1.0 + nrm(ks[1], (L, D), 0.05)
    w_in = nrm(ks[2], (L, D, N_IN), D ** -0.5)
    b_forget = 2.0 + nrm(ks[3], (L, H), 0.5)
    rwkv_mu = jax.random.uniform(ks[4], (L, N_RWKV_SHIFT), f32)
    rwkv_w0 = jax.random.uniform(ks[5], (L, G), f32, -6.0, 0.0)
    rwkv_w2 = nrm(ks[6], (L, R, G), 0.1 * R ** -0.5)
    rwkv_a0 = nrm(ks[7], (L, G), 0.1)
    rwkv_a2 = nrm(ks[8], (L, R, G), 0.1 * R ** -0.5)
    rwkv_k_k = 0.85 + nrm(ks[9], (L, G), 0.05)
    rwkv_k_a = 1.0 + nrm(ks[10], (L, G), 0.05)
    rwkv_r_k = nrm(ks[11], (L, H, N), 0.1)
    rwkv_ln_g = 1.0 + nrm(ks[12], (L, G), 0.05)
    rwkv_ln_b = nrm(ks[13], (L, G), 0.02)
    lru_conv_w = nrm(ks[14], (L, CONV_WIDTH, G), CONV_WIDTH ** -0.5)
    lru_conv_b = nrm(ks[15], (L, G), 0.02)
    lru_w_a = nrm(ks[16], (L, H, N, N), N ** -0.5)
    lru_b_a = nrm(ks[17], (L, G), 0.02)
    lru_w_x = nrm(ks[18], (L, H, N, N), N ** -0.5)
    lru_b_x = nrm(ks[19], (L, G), 0.02)
    a_pow_c = jax.random.uniform(ks[20], (L, G), f32, 0.9, 0.999)
    a_base = a_pow_c ** (1.0 / LRU_C)
    lru_lambda = jnp.log(a_base) - jnp.log1p(-a_base)
    w_out = nrm(ks[21], (L, D_MIX, D), D_MIX ** -0.5)
    final_g = 1.0 + nrm(ks[22], (D,), 0.05)
    return {"x": x, "norm_g": norm_g, "w_in": w_in, "b_forget": b_forget,
            "rwkv_mu": rwkv_mu, "rwkv_w0": rwkv_w0, "rwkv_w2": rwkv_w2,
            "rwkv_a0": rwkv_a0, "rwkv_a2": rwkv_a2, "rwkv_k_k": rwkv_k_k,
            "rwkv_k_a": rwkv_k_a, "rwkv_r_k": rwkv_r_k, "rwkv_ln_g": rwkv_ln_g,
            "rwkv_ln_b": rwkv_ln_b, "lru_conv_w": lru_conv_w, "lru_conv_b": lru_conv_b,
            "lru_w_a": lru_w_a, "lru_b_a": lru_b_a, "lru_w_x": lru_w_x,
            "lru_b_x": lru_b_x, "lru_lambda": lru_lambda, "w_out": w_out,
            "final_g": final_g}


def reference(x, norm_g, w_in, b_forget, rwkv_mu, rwkv_w0, rwkv_w2, rwkv_a0, rwkv_a2,
              rwkv_k_k, rwkv_k_a, rwkv_r_k, rwkv_ln_g, rwkv_ln_b, lru_conv_w, lru_conv_b,
              lru_w_a, lru_b_a, lru_w_x, lru_b_x, lru_lambda, w_out, final_g):
    f32 = jnp.float32
    for l in range(DEPTH):
        h = rms_norm(x, norm_g[l])
        p = h @ w_in[l]
        (fq, fk, fv, fg, ff, sq, sk, sv, sg, rw, rg, lx, lg) = split_columns(p)
        log_f = jax.nn.log_sigmoid(ff.astype(f32) + b_forget[l]).transpose(0, 2, 1)
        y_fox = forgetting_attention(to_heads(fq), to_heads(fk), to_heads(fv), log_f)
        y_sb = stick_breaking_attention(to_heads(sq), to_heads(sk), to_heads(sv))
        y_rw = rwkv7_time_mix(rw, rwkv_mu[l], rwkv_w0[l], rwkv_w2[l], rwkv_a0[l], rwkv_a2[l],
                              rwkv_k_k[l], rwkv_k_a[l], rwkv_r_k[l], rwkv_ln_g[l], rwkv_ln_b[l])
        y_lru = rg_lru(lx, lru_conv_w[l], lru_conv_b[l], lru_w_a[l], lru_b_a[l],
                       lru_w_x[l], lru_b_x[l], lru_lambda[l])
        y = jnp.concatenate([
            y_fox * jax.nn.silu(fg.astype(f32)),
            y_sb * jax.nn.silu(sg.astype(f32)),
            y_rw * jax.nn.silu(rg.astype(f32)),
            y_lru * jax.nn.silu(lg.astype(f32)),
        ], axis=-1).astype(x.dtype)
        x = x + y @ w_out[l]
    return rms_norm(x, final_g)
```

```python
import numpy as np
from contextlib import ExitStack
import concourse.bass as bass
import concourse.mybir as mybir
from concourse.bass_utils import run_bass_kernel_spmd

F32 = mybir.dt.float32
BF16 = mybir.dt.bfloat16
AF = mybir.ActivationFunctionType
ALU = mybir.AluOpType

D = 1024
G = 256
HD = 64
R = 32
DEPTH = 2
BATCH = 2
SEQ = 8192
N_IN = 3652
TS = 512
CH = 64
NCH = TS // CH
RMS_EPS = 1e-6
GN_EPS = 64e-5
LRU_C = 8.0
NEG = -30000.0


class TT:
    __slots__ = ("name", "last_w", "readers")

    def __init__(self, name):
        self.name = name
        self.last_w = None
        self.readers = {}


class Prog:
    ENG = ("pe", "act", "dve", "pool", "sp")
    NDS = 24

    def __init__(self, nc, es):
        self.nc = nc
        self.es = es
        self.semh = {}
        for e in self.ENG:
            self.semh["e:" + e] = es.enter_context(nc.semaphore("sem_" + e))
        for i in range(self.NDS):
            self.semh["d:%d" % i] = es.enter_context(nc.semaphore("dsem%d" % i))
        self.cnt = {e: 0 for e in self.ENG}
        self.ops = {e: [] for e in self.ENG}
        self.seen = {e: {} for e in self.ENG}
        self.dma_cnt = [0] * self.NDS
        self.dma_rr = 0
        self.nops = 0
        self.semh["cc"] = es.enter_context(nc.semaphore("sem_cc"))
        self.cc_cnt = 0

    def op(self, eng, fn, reads=(), writes=(), dma=False, cc=False):
        waits = {}

        def need(ev, same_ok):
            if ev is None:
                return
            skey, val, src = ev
            if same_ok and src == eng:
                return
            if waits.get(skey, 0) < val:
                waits[skey] = val

        for t in reads:
            need(t.last_w, eng == "pe")
        for t in writes:
            need(t.last_w, True)
            for skey, (val, src) in t.readers.items():
                need((skey, val, src), True)
        if dma:
            i = self.dma_rr
            self.dma_rr = (i + 1) % self.NDS
            k = self.dma_cnt[i]
            if k > 0:
                need(("d:%d" % i, 16 * k, "dma"), False)
            self.dma_cnt[i] = k + 1
            ev = ("d:%d" % i, 16 * (k + 1), "dma")
            inc = 16
        elif cc:
            self.cc_cnt += 1
            ev = ("cc", self.cc_cnt, "cc")
            inc = None
        else:
            self.cnt[eng] += 1
            ev = ("e:" + eng, self.cnt[eng], eng)
            inc = 1
        seen = self.seen[eng]
        wl = []
        for skey, val in waits.items():
            if seen.get(skey, 0) < val:
                seen[skey] = val
                wl.append((skey, val))
        self.ops[eng].append((wl, fn, ev[0], inc))
        self.nops += 1
        for t in reads:
            old = t.readers.get(ev[0])
            if old is None or old[0] < ev[1]:
                t.readers[ev[0]] = (ev[1], ev[2])
        for t in writes:
            t.last_w = ev
            t.readers = {}
        return ev

    def finish(self):
        wl = []
        for i in range(self.NDS):
            if self.dma_cnt[i] > 0:
                wl.append(("d:%d" % i, 16 * self.dma_cnt[i]))
        for e in self.ENG:
            if e != "sp" and self.cnt[e] > 0:
                wl.append(("e:" + e, self.cnt[e]))
        if self.cc_cnt > 0:
            wl.append(("cc", self.cc_cnt))
        self.ops["sp"].append((wl, None, None, 0))

    def emit(self):
        nc = self.nc
        with nc.Block() as block:
            def mk(e):
                def body(engh):
                    for wl, fn, skey, inc in self.ops[e]:
                        for k, v in wl:
                            engh.wait_ge(self.semh[k], v)
                        if fn is not None:
                            ins = fn(engh)
                            if inc is None:
                                ins.then_inc(self.semh[skey])
                            else:
                                ins.then_inc(self.semh[skey], inc)
                return body
            block.tensor(mk("pe"))
            block.scalar(mk("act"))
            block.vector(mk("dve"))
            block.gpsimd(mk("pool"))
            block.sync(mk("sp"))

    def mm(self, out, lhsT, rhs, start, stop, r, w):
        return self.op("pe", lambda e: e.matmul(out, lhsT, rhs, start=start, stop=stop), r, w)

    def tr(self, out, in_, ident, r, w):
        return self.op("pe", lambda e: e.transpose(out, in_, ident), r, w)

    def act(self, out, in_, func, r, w, bias=0.0, scale=1.0):
        return self.op("act", lambda e: e.activation(out=out, in_=in_, func=func, bias=bias, scale=scale), r, w)

    def tt(self, eng, out, in0, in1, op, r, w):
        return self.op(eng, lambda e: e.tensor_tensor(out=out, in0=in0, in1=in1, op=op), r, w)

    def ts(self, eng, out, in0, s1, op0, r, w, s2=None, op1=None):
        if op1 is None:
            return self.op(eng, lambda e: e.tensor_scalar(out=out, in0=in0, scalar1=s1, scalar2=None, op0=op0), r, w)
        return self.op(eng, lambda e: e.tensor_scalar(out=out, in0=in0, scalar1=s1, scalar2=s2, op0=op0, op1=op1), r, w)

    def stt(self, out, in0, scalar, in1, op0, op1, r, w):
        return self.op("dve", lambda e: e.scalar_tensor_tensor(out=out, in0=in0, scalar=scalar, in1=in1, op0=op0, op1=op1), r, w)

    def cp(self, eng, out, in_, r, w):
        if eng == "act":
            return self.op("act", lambda e: e.copy(out=out, in_=in_), r, w)
        return self.op(eng, lambda e: e.tensor_copy(out=out, in_=in_), r, w)

    def memset(self, eng, ap, val, w):
        return self.op(eng, lambda e: e.memset(ap, val), (), w)

    def scan(self, out, d0, d1, init, op0, op1, r, w):
        return self.op("dve", lambda e: e.tensor_tensor_scan(out=out, data0=d0, data1=d1, initial=init, op0=op0, op1=op1), r, w)

    def recip(self, out, in_, r, w):
        return self.op("dve", lambda e: e.reciprocal(out=out, in_=in_), r, w)

    def dma(self, out, in_, r, w, eng="sp"):
        return self.op(eng, lambda e: e.dma_start(out=out, in_=in_), r, w, dma=True)


class Buf:
    def __init__(self, t, name):
        self.t = t
        self.tt = TT(name)

    def __getitem__(self, k):
        return self.t[k]


C_ID = 0
C_FOXM = C_ID + 128
C_M128 = C_FOXM + 128
C_NUINC = C_M128 + 128
C_NLOW = C_NUINC + 128
C_ONES = C_NLOW + 128
C_M320 = C_ONES + 128
C_RST = C_M320 + 320
C_ONE5 = C_RST + 512
NCONST = C_ONE5 + 512


def make_consts():
    c = np.zeros((128, NCONST), np.float32)
    i = np.arange(128)[:, None]
    j = np.arange(128)[None, :]
    c[:, C_ID:C_ID + 128] = (i == j)
    c[:, C_FOXM:C_FOXM + 128] = np.where(i <= j, 0.0, NEG)
    c[:, C_M128:C_M128 + 128] = (j > i)
    c[:, C_NUINC:C_NUINC + 128] = -1.0 * (i >= j)
    c[:, C_NLOW:C_NLOW + 128] = -1.0 * (i < j)
    c[:, C_ONES:C_ONES + 128] = 1.0
    a = np.arange(64)[:, None]
    b = np.arange(64)[None, :]
    m = np.zeros((64, 320), np.float32)
    m[:, 0:64] = (b > a)
    m[:, 64:128] = (b >= a)
    m[:, 128:192] = (b > a)
    m[:, 192:256] = (b >= a)
    m[:, 256:320] = (b < a)
    c[0:64, C_M320:C_M320 + 320] = m
    rst = np.ones((512,), np.float32)
    rst[::64] = 0.0
    c[:, C_RST:C_RST + 512] = rst[None, :]
    c[:, C_ONE5:C_ONE5 + 512] = 1.0
    return c


V_G = 0
V_MU_R = 8
V_MU_K = 9
V_MU_V = 10
V_MU_WA = 11
V_W0 = 12
V_A0 = 13
V_KK = 14
V_KA = 15
V_RK = 16
V_LNG = 17
V_LNB = 18
V_CW = 19
V_CB = 23
V_BA = 24
V_BX = 25
V_LAM = 26
V_BF = 27
V_FG = 28
NVEC = 36

CG_QQ = 0
CG_KK = 128
CG_GA = 256
CG_GB = 384
CG_RK = 512
CG_VWA = 640
CG_LX = 768
CG_FF = 832
CG_VV = 833
NCOL = 961


def col_index(h):
    g = lambda base: list(range(base + 64 * h, base + 64 * h + 64))
    fq, fk, fv, fg = g(0), g(256), g(512), g(768)
    ff = [1024 + h]
    sq, sk, sv, sg = g(1028), g(1284), g(1540), g(1796)
    rr, rk, rv = g(2052), g(2308), g(2564)
    wl = list(range(2820, 2852))
    al = list(range(2852, 2884))
    rg, lx, lg = g(2884), g(3140), g(3396)
    cols = fq + sq + fk + sk + fg + sg + rg + lg + rr + rk + rv + wl + al + lx + ff + fv + sv
    assert len(cols) == NCOL
    return np.array(cols)


def pack_layer_inputs(inp, l, h):
    f = np.float32
    hs = slice(64 * h, 64 * h + 64)
    win = np.ascontiguousarray(inp["w_in"][l][:, col_index(h)]).astype(f)
    vecs = np.zeros((128, NVEC), f)
    vecs[:, V_G:V_G + 8] = inp["norm_g"][l].reshape(8, 128).T
    mu = inp["rwkv_mu"][l]
    vecs[0:64, V_MU_R] = mu[0:256][hs]
    vecs[0:64, V_MU_K] = mu[256:512][hs]
    vecs[0:64, V_MU_V] = mu[512:768][hs]
    vecs[0:32, V_MU_WA] = mu[768:800]
    vecs[32:64, V_MU_WA] = mu[800:832]
    vecs[0:64, V_W0] = inp["rwkv_w0"][l][hs]
    vecs[0:64, V_A0] = inp["rwkv_a0"][l][hs]
    vecs[0:64, V_KK] = inp["rwkv_k_k"][l][hs]
    vecs[0:64, V_KA] = inp["rwkv_k_a"][l][hs]
    vecs[0:64, V_RK] = inp["rwkv_r_k"][l][h]
    vecs[0:64, V_LNG] = inp["rwkv_ln_g"][l][hs]
    vecs[0:64, V_LNB] = inp["rwkv_ln_b"][l][hs]
    for i in range(4):
        vecs[0:64, V_CW + i] = inp["lru_conv_w"][l][i][hs]
    vecs[0:64, V_CB] = inp["lru_conv_b"][l][hs]
    vecs[0:64, V_BA] = inp["lru_b_a"][l][hs]
    vecs[0:64, V_BX] = inp["lru_b_x"][l][hs]
    vecs[0:64, V_LAM] = inp["lru_lambda"][l][hs]
    vecs[0, V_BF] = inp["b_forget"][l][h]
    vecs[:, V_FG:V_FG + 2] = inp["final_g"][256 * h:256 * h + 256].reshape(2, 128).T
    wsm = np.zeros((64, 256), f)
    wsm[0:32, 0:64] = inp["rwkv_w2"][l][:, hs]
    wsm[32:64, 64:128] = inp["rwkv_a2"][l][:, hs]
    wsm[:, 128:192] = inp["lru_w_a"][l][h]
    wsm[:, 192:256] = inp["lru_w_x"][l][h]
    return {"win": win, "vecs": vecs, "wsm": wsm}


def build_fused(S, parts=("rwkv", "lru", "fox", "sb"), nlayers=DEPTH, final=True, groups=((0, 1, 2, 3), (4, 5, 6, 7))):
    NT = S // TS
    NB = S // 128
    nc = bass.Bass("TRN2", target_bir_lowering=False)
    es = ExitStack()
    P = Prog(nc, es)

    def dram(name, shape, dt, kind):
        return nc.dram_tensor(name, shape, dt, kind=kind).ap()

    xT_d = dram("xT", [D, S], F32, "ExternalInput")
    win_ds = [dram("win%d" % l, [D, NCOL], F32, "ExternalInput") for l in range(nlayers)]
    vecs_ds = [dram("vecs%d" % l, [128, NVEC], F32, "ExternalInput") for l in range(nlayers)]
    wsm_ds = [dram("wsm%d" % l, [64, 256], F32, "ExternalInput") for l in range(nlayers)]
    woutp_ds = [dram("woutp%d" % l, [D, D], F32, "ExternalInput") for l in range(nlayers)]
    consts_d = dram("consts", [128, NCONST], F32, "ExternalInput")
    out_d = dram("outT", [256, S], F32, "ExternalOutput")
    xfin_d = dram("xfin", [256, S], F32, "ExternalInput")
    woutf_ds = [dram("woutf%d" % l, [D, 256], F32, "ExternalInput") for l in range(nlayers)]
    x2_d = dram("x2_i", [256, S], F32, "Internal")
    ssq_d = dram("ssq_i", [1, S], F32, "Internal")
    ssqr_d = dram("ssqr_i", [1, S], F32, "Internal")
    CQ = min(4, NT)
    NQ = NT // CQ
    yown_ds = [[dram("yown%d_%d" % (l, q), [256, CQ * TS], BF16, "Internal") for q in range(NQ)] for l in range(nlayers)]
    yg_ds = [[dram("yg%d_%d" % (l, q), [D, CQ * TS], BF16, "Internal") for q in range(NQ)] for l in range(nlayers)]
    yown_tt = [[(TT("yo%d_%d_a" % (l, i)), TT("yo%d_%d_b" % (l, i))) for i in range(NT)] for l in range(nlayers)]
    yg_tt = [[TT("yg%d_%d" % (l, q)) for q in range(NQ)] for l in range(nlayers)]

    def yg_tile(l, i):
        return yg_ds[l][i // CQ][:, (i % CQ) * TS:(i % CQ + 1) * TS].rearrange("(k p) t -> p k t", p=128)
    with_prev = True

    sb_used = [0]

    def sb(name, shape, dt=F32):
        n = 1
        for d_ in shape[1:]:
            n *= d_
        sb_used[0] += ((n * (2 if dt == BF16 else 4) + 31) // 32) * 32
        return Buf(es.enter_context(nc.sbuf_tensor(name, shape, dt)), name)

    def psb(name, shape, dt=F32):
        return Buf(es.enter_context(nc.psum_tensor(name, shape, dt)), name)

    cst = sb("cst", [128, NCONST])
    cbf = sb("cbf", [128, C_M320], BF16)
    vec = sb("vec", [128, NVEC])
    vx = sb("vx", [128, 16])
    wsmb = sb("wsm_b", [64, 256], BF16)
    wbf = sb("wbf", [128, 8, D], BF16)
    woutb = sb("woutb", [128, 8, D], BF16)

    P.dma(cst[:, :], consts_d[:, :], [], [cst.tt])
    P.cp("dve", cbf[:, :], cst[:, 0:C_M320], [cst.tt], [cbf.tt])
    ident = cbf[:, C_ID:C_ID + 128]
    onesb = cbf[:, C_ONES:C_ONES + 128]
    onesf = cst[:, C_ONES:C_ONES + 128]
    X_NW0, X_NA0, X_1MKA, X_CL, X_2CL, X_NBA, X_NBX, X_NBF, X_T = 0, 1, 2, 3, 4, 5, 6, 7, 8

    def load_small(l):
        P.dma(vec[:, :], vecs_ds[l][:, :], [], [vec.tt])
        P.dma(t1[:, 0:256], wsm_ds[l][:, :], [], [t1.tt])
        P.cp("dve", wsmb[:, :], t1[:, 0:256], [t1.tt], [wsmb.tt])
        P.ts("dve", vx[0:64, X_NW0:X_NW0 + 1], vec[0:64, V_W0:V_W0 + 1], -1.0, ALU.mult, [vec.tt], [vx.tt])
        P.ts("dve", vx[0:64, X_1MKA:X_1MKA + 1], vec[0:64, V_KA:V_KA + 1], -1.0, ALU.mult, [vec.tt], [vx.tt], 1.0, ALU.add)
        P.act(vx[0:64, X_T:X_T + 1], vec[0:64, V_LAM:V_LAM + 1], AF.Exp, [vec.tt], [vx.tt], scale=-1.0)
        P.act(vx[0:64, X_T:X_T + 1], vx[0:64, X_T:X_T + 1], AF.Ln, [vx.tt], [vx.tt], bias=1.0)
        P.ts("dve", vx[0:64, X_CL:X_CL + 1], vx[0:64, X_T:X_T + 1], -LRU_C, ALU.mult, [vx.tt], [vx.tt])
        P.ts("dve", vx[0:64, X_2CL:X_2CL + 1], vx[0:64, X_T:X_T + 1], -2.0 * LRU_C, ALU.mult, [vx.tt], [vx.tt])
        P.ts("dve", vx[0:1, X_NBF:X_NBF + 1], vec[0:1, V_BF:V_BF + 1], -1.0, ALU.mult, [vec.tt], [vx.tt])
        P.ts("dve", vx[0:64, X_NA0:X_NA0 + 1], vec[0:64, V_A0:V_A0 + 1], -1.0, ALU.mult, [vec.tt], [vx.tt])
        P.ts("dve", vx[0:64, X_NBA:X_NBA + 1], vec[0:64, V_BA:V_BA + 1], -1.0, ALU.mult, [vec.tt], [vx.tt])
        P.ts("dve", vx[0:64, X_NBX:X_NBX + 1], vec[0:64, V_BX:V_BX + 1], -1.0, ALU.mult, [vec.tt], [vx.tt])

    ps = [psb("ps%d" % i, [128, 512]) for i in range(7)]
    psT = psb("psT", [128, 1024], BF16)
    rot = {"proj": [0, 1], "sc": [2, 3], "pa": [0, 2, 3]}
    rot_i = {"proj": 0, "sc": 0, "pa": 0}

    def bank(kind):
        l = rot[kind]
        b = ps[l[rot_i[kind] % len(l)]]
        rot_i[kind] += 1
        return b

    PS_O, PS_R, PS_C = ps[4], ps[5], ps[6]

    kaug = sb("kaug", [128, S], BF16)
    kaug_tt = [TT("kaug%d" % i) for i in range(NT)]
    ksb = sb("ksb", [128, S], BF16)
    ksb_tt = [TT("ksb%d" % i) for i in range(NT)]
    vfox = sb("vfox", [128, NB, 65], BF16)
    vfox_tt = [TT("vfox%d" % i) for i in range(NT)]
    vsb = sb("vsb", [128, NB, 65], BF16)
    vfoxf = vfox[:, :, :].rearrange("p b c -> p (b c)")
    vsbf = vsb[:, :, :].rearrange("p b c -> p (b c)")
    vsb_tt = [TT("vsb%d" % i) for i in range(NT)]
    P.memset("pool", kaug[64:128, :], 0.0, kaug_tt)
    P.memset("pool", kaug[64:70, :], 1.0, kaug_tt)
    P.memset("pool", ksb[64:128, :], 0.0, ksb_tt)
    P.memset("pool", vsb[:, :, 64:65], 0.0, vsb_tt)
    P.memset("pool", vfox[:, :, 64:65], 1.0, vfox_tt)

    xt = sb("xt", [128, 8, TS])
    xtw = xt[:, :, :].rearrange("p k t -> p (k t)")

    gnx = sb("gnx", [128, 8])

    def load_win(l):
        P.dma(gnx[:, :], vecs_ds[l][:, V_G:V_G + 8], [], [gnx.tt])
        for k in range(8):
            st = xtw[:, (k % 4) * 1024:(k % 4) * 1024 + NCOL]
            P.dma(st, win_ds[l][k * 128:(k + 1) * 128, :], [], [xt.tt])
            P.ts("dve", wbf[:, k, 0:NCOL], st, gnx[:, k:k + 1], ALU.mult, [xt.tt, gnx.tt], [wbf.tt])
            yield

    def load_wout(dst, l):
        for k in range(8):
            st = xtw[:, (k % 4) * 1024:(k % 4) * 1024 + D]
            P.dma(st, woutp_ds[l][k * 128:(k + 1) * 128, :], [], [xt.tt])
            P.cp("dve", dst[:, k, :], st, [xt.tt], [dst.tt])
            yield

    def load_woutf():
        for l, dst in enumerate((woutb, wbf)):
            for k in range(8):
                st = xtw[:, (k % 4) * 1024:(k % 4) * 1024 + 256]
                P.dma(st, woutf_ds[l][k * 128:(k + 1) * 128, :], [], [xt.tt])
                P.cp("dve", dst[:, k, 0:256], st, [xt.tt], [dst.tt])
                yield

    def prefetch_next(l):
        if l + 1 < nlayers:
            yield from load_win(l + 1)
            yield from load_wout(woutb, l)
        else:
            yield from load_woutf()

    xn = sb("xn", [128, 8, TS], BF16)
    ypb = sb("ypb", [128, 8, TS], BF16)
    qaug = sb("qaug", [128, TS], BF16)
    P.memset("pool", qaug[64:128, :], 0.0, [qaug.tt])
    P.memset("pool", qaug[64:70, :], 1.0, [qaug.tt])
    qsb = sb("qsb", [128, TS], BF16)
    P.memset("pool", qsb[64:128, :], 0.0, [qsb.tt])
    sgf = sb("sgf", [64, TS], BF16)
    sgs = sb("sgs", [64, TS], BF16)
    sgr = sb("sgr", [64, TS], BF16)
    sgl = sb("sgl", [64, TS], BF16)
    FS = sb("FS", [128, TS])
    FB = sb("FB", [128, TS], BF16)
    FB2 = sb("FB2", [128, TS], BF16)
    P.memset("pool", FS[:, :], 0.0, [FS.tt])
    P.memset("pool", FB[:, :], 0.0, [FB.tt])
    fcar = sb("fcar", [1, 2])
    P.memset("dve", fcar[:, :], 0.0, [fcar.tt])
    y01 = sb("y01", [128, TS], BF16)
    y23 = sb("y23", [128, TS], BF16)

    ebuf = [sb("ebuf%d" % i, [128, TS], BF16) for i in range(3)]
    fscr = sb("fscr", [128, TS])
    spb = [sb("spb%d" % i, [128, TS], BF16) for i in range(2)]
    gbuf = [sb("gbuf%d" % i, [128, TS], BF16) for i in range(2)]
    abuf = [sb("abuf%d" % i, [128, TS], BF16) for i in range(2)]
    pT = [sb("pT%d" % i, [128, TS], BF16) for i in range(2)]
    ebuf3 = ebuf
    rrow = fscr
    xsq = spb

    def sb64(name, cols=TS, dt=F32):
        return sb(name, [64, cols], dt)

    class ColView:
        def __init__(self, buf, off, n):
            self.buf, self.off, self.n, self.tt = buf, off, n, buf.tt

        def __getitem__(self, key):
            rs, cs = key
            st = 0 if cs.start is None else cs.start
            en = self.n if cs.stop is None else cs.stop
            return self.buf.t[rs, self.off + st:self.off + en]

    rbuf = sb64("rbuf", TS + 1)
    kbuf = sb64("kbuf", TS + 1)
    vbuf = sb64("vbuf", TS + 1)
    wabuf = sb64("wabuf", TS + 1)
    for b_ in (rbuf, kbuf, vbuf, wabuf):
        P.memset("pool", b_[:, 0:1], 0.0, [b_.tt])
    r_s = ColView(rbuf, 1, TS)
    k_s = ColView(kbuf, 1, TS)
    v_s = ColView(vbuf, 1, TS)
    wa_s = ColView(wabuf, 1, TS)
    tmpd = sb64("tmpd")
    wa_b = sb64("wa_b", TS, BF16)
    v_b = sb64("v_b", TS, BF16)
    t1 = sb64("t1")
    t2 = sb64("t2")
    ew = sb64("ew")
    cew = sb64("cew")
    alpha = sb64("alpha")
    kkn = sb64("kkn")
    kmod = sb64("kmod")
    kal = sb64("kal")
    e1 = sb64("e1")
    e2 = sb64("e2")
    e3 = tmpd
    e4 = sb64("e4")
    pc = sb("pc", [64, NCH])
    AR = sb("AR", [64, NCH, 2, CH], BF16)
    btl = sb64("btl", TS, BF16)
    ktl = sb64("ktl", TS, BF16)
    bhat = sb64("bhat", TS, BF16)
    khat = sb64("khat", TS, BF16)
    tok = sb("tok", [64, NCH, 4, CH], BF16)
    amat = sb("amat", [64, NCH, 320], BF16)
    XX = [sb("XX%d" % i, [64, NCH, 128], BF16) for i in range(2)]
    TTm = [sb("TTm%d" % i, [64, NCH, CH], BF16) for i in range(2)]
    Mst = sb("Mst", [64, 64])
    Mbf = sb("Mbf", [64, 64], BF16)
    P.memset("dve", Mst[:, :], 0.0, [Mst.tt])
    P.memset("dve", Mbf[:, :], 0.0, [Mbf.tt])
    Gbf = sb("Gbf", [64, 64], BF16)
    Ubf = sb("Ubf", [64, 64], BF16)
    yrw = k_s
    bonus = alpha
    rcp = e4

    lxb = sb64("lxb", TS + 3)
    P.memset("pool", lxb[:, 0:3], 0.0, [lxb.tt])
    xc = t1
    xcb = wa_b
    lr = t2
    li = e1
    la = e2
    lu = tmpd
    lh = e4
    lcar = sb("lcar", [64, 1])
    P.memset("dve", lcar[:, :], 0.0, [lcar.tt])

    vcol = lambda c, n=64: vec[0:n, c:c + 1]
    xcol = lambda c, n=64: vx[0:n, c:c + 1]

    def load_x(i):
        for k in range(8):
            P.dma(xt[:, k, :], xT_d[k * 128:(k + 1) * 128, i * TS:(i + 1) * TS], [], [xt.tt])

    def proj_fm(cg, M, kind="proj"):
        b = bank(kind)
        for k in range(8):
            P.mm(b[0:M, :], wbf[:, k, cg:cg + M], xn[:, k, :], k == 0, k == 7, [wbf.tt, xn.tt], [b.tt])
        return b

    def c3(ap):
        return ap.rearrange("p (c t) -> p c t", c=NCH)

    def sigmoid_exp(dst, dtt, src, stt_, nbias):
        rd = [stt_] + ([vx.tt] if not isinstance(nbias, float) else [])
        P.act(dst, src, AF.Exp, rd, [dtt], bias=nbias, scale=-1.0)
        P.act(dst, dst, AF.Ln, [dtt], [dtt], bias=1.0)
        P.act(dst, dst, AF.Exp, [dtt], [dtt], scale=-1.0)

    def shift(buf, mu_col, tmp):
        P.tt("pool", tmp[:, :], buf[:, 0:TS], buf[:, 1:TS + 1], ALU.subtract, [buf.tt], [tmp.tt])
        P.cp("pool", buf[:, 0:1], buf[:, TS:TS + 1], [buf.tt], [buf.tt])
        P.stt(buf[:, 1:TS + 1], tmp[:, :], vcol(mu_col), buf[:, 1:TS + 1], ALU.mult, ALU.add,
              [tmp.tt, vec.tt, buf.tt], [buf.tt])

    def rwkv_tile(i):
        shift(wabuf, V_MU_WA, tmpd)
        shift(kbuf, V_MU_K, e2)
        yield
        shift(rbuf, V_MU_R, e4)
        shift(vbuf, V_MU_V, t2)
        yield
        P.cp("dve", wa_b[:, :], wa_s[:, :], [wa_s.tt], [wa_b.tt])
        P.act(t1[0:32, :], wa_s[0:32, :], AF.Exp, [wa_s.tt], [t1.tt], scale=-2.0)
        P.act(t1[0:32, :], t1[0:32, :], AF.Ln, [t1.tt], [t1.tt], bias=1.0)
        P.act(t1[0:32, :], t1[0:32, :], AF.Exp, [t1.tt], [t1.tt], scale=-1.0)
        P.ts("dve", wa_b[0:32, :], t1[0:32, :], 2.0, ALU.mult, [t1.tt], [wa_b.tt], -1.0, ALU.add)
        P.cp("pool", v_b[:, :], v_s[:, :], [v_s.tt], [v_b.tt])
        yield
        bw = ps[1]
        P.mm(bw[0:64, :], wsmb[:, 0:64], wa_b[:, :], True, True, [wsmb.tt, wa_b.tt], [bw.tt])
        P.act(e1[:, :], bw[0:64, :], AF.Exp, [bw.tt, vx.tt], [e1.tt], bias=xcol(X_NW0), scale=-1.0)
        ba = ps[1]
        P.mm(ba[0:64, :], wsmb[:, 64:128], wa_b[:, :], True, True, [wsmb.tt, wa_b.tt], [ba.tt])
        sigmoid_exp(alpha[:, :], alpha.tt, ba[0:64, :], ba.tt, xcol(X_NA0))
        yield
        P.act(t1[:, :], e1[:, :], AF.Ln, [e1.tt], [t1.tt], bias=1.0)
        P.act(ew[:, :], t1[:, :], AF.Exp, [t1.tt], [ew.tt], bias=-0.5, scale=-1.0)
        P.scan(cew[:, :], cst[0:64, C_RST:C_RST + TS], ew[:, :], 0.0, ALU.mult, ALU.add,
               [cst.tt, ew.tt], [cew.tt])
        P.ts("dve", t2[:, :], k_s[:, :], vcol(V_KK), ALU.mult, [k_s.tt, vec.tt], [t2.tt])
        P.tt("pool", e2[:, :], t2[:, :], t2[:, :], ALU.mult, [t2.tt], [e2.tt])
        yield
        b = ps[1]
        P.mm(b[0:64, :], onesf[0:64, 0:64], e2[:, :], True, True, [cst.tt, e2.tt], [b.tt])
        P.ts("dve", e3[:, :], b[0:64, :], 1e-12, ALU.max, [b.tt], [e3.tt])
        yield
        P.act(e3[:, :], e3[:, :], AF.Ln, [e3.tt], [e3.tt])
        P.act(e3[:, :], e3[:, :], AF.Exp, [e3.tt], [e3.tt], scale=-0.5)
        P.tt("dve", kkn[:, :], t2[:, :], e3[:, :], ALU.mult, [t2.tt, e3.tt], [kkn.tt])
        P.ts("dve", e4[:, :], alpha[:, :], vcol(V_KA), ALU.mult, [alpha.tt, vec.tt, vx.tt], [e4.tt],
             xcol(X_1MKA), ALU.add)
        P.tt("dve", kmod[:, :], k_s[:, :], e4[:, :], ALU.mult, [k_s.tt, e4.tt], [kmod.tt])
        P.tt("pool", kal[:, :], kkn[:, :], alpha[:, :], ALU.mult, [kkn.tt, alpha.tt], [kal.tt])
        P.stt(t2[:, :], r_s[:, :], vcol(V_RK), kmod[:, :], ALU.mult, ALU.mult, [r_s.tt, vec.tt, kmod.tt], [t2.tt])
        yield
        b = ps[1]
        P.mm(b[0:64, :], onesf[0:64, 0:64], t2[:, :], True, True, [cst.tt, t2.tt], [b.tt])
        P.tt("dve", bonus[:, :], b[0:64, :], v_s[:, :], ALU.mult, [b.tt, v_s.tt], [bonus.tt])
        yield
        cew3 = c3(cew[:, :])
        cend = cew3[:, :, CH - 1:CH].to_broadcast([64, NCH, CH])
        P.act(e1[:, :], cew[:, :], AF.Exp, [cew.tt], [e1.tt], scale=-1.0)
        P.tt("dve", AR[:, :, 1, :], c3(r_s[:, :]), c3(e1[:, :]), ALU.mult, [r_s.tt, e1.tt], [AR.tt])
        P.act(e2[:, :], cew[:, :], AF.Exp, [cew.tt], [e2.tt])
        P.tt("dve", btl[:, :], kal[:, :], e2[:, :], ALU.mult, [kal.tt, e2.tt], [btl.tt])
        P.tt("pool", ktl[:, :], kmod[:, :], e2[:, :], ALU.mult, [kmod.tt, e2.tt], [ktl.tt])
        yield
        P.tt("pool", e3[:, :], cew[:, :], ew[:, :], ALU.subtract, [cew.tt, ew.tt], [e3.tt])
        P.act(e3[:, :], e3[:, :], AF.Exp, [e3.tt], [e3.tt], scale=-1.0)
        P.stt(AR[:, :, 0, :], c3(kkn[:, :]), -1.0, c3(e3[:, :]), ALU.mult, ALU.mult, [kkn.tt, e3.tt], [AR.tt])
        yield
        P.tt("dve", c3(e4[:, :]), cend, cew3, ALU.subtract, [cew.tt], [e4.tt])
        P.act(e4[:, :], e4[:, :], AF.Exp, [e4.tt], [e4.tt], scale=-1.0)
        P.tt("dve", bhat[:, :], kal[:, :], e4[:, :], ALU.mult, [kal.tt, e4.tt], [bhat.tt])
        P.tt("pool", khat[:, :], kmod[:, :], e4[:, :], ALU.mult, [kmod.tt, e4.tt], [khat.tt])
        P.act(pc[:, :], cew3[:, :, CH - 1], AF.Exp, [cew.tt], [pc.tt], scale=-1.0)

    def rwkv_stream(i, flags=None):
        B_RW = ps[1]
        id64 = cbf[0:64, C_ID:C_ID + 64]
        for hh in range(2):
            for cl in range(4):
                c = hh * 4 + cl
                cs = slice(c * CH, (c + 1) * CH)
                for a_, src in enumerate((bhat, khat, v_b)):
                    o = (cl * 4 + a_) * CH
                    P.tr(psT[0:64, o:o + CH], src[:, cs], id64, [src.tt, cbf.tt], [psT.tt])
                o = (cl * 4 + 3) * CH
                P.tr(psT[0:64, o:o + CH], AR[:, c, 0, :], id64, [AR.tt, cbf.tt], [psT.tt])
            P.cp("act", tok[:, hh * 4:hh * 4 + 4, :, :].rearrange("p c a t -> p (c a t)"), psT[0:64, 0:1024],
                 [psT.tt], [tok.tt])
            yield
        for c in range(NCH):
            cs = slice(c * CH, (c + 1) * CH)
            b = B_RW
            arc = AR[:, c, :, :].rearrange("p a t -> p (a t)")
            P.mm(b[0:64, 0:128], btl[:, cs], arc, True, True, [btl.tt, AR.tt], [b.tt])
            P.mm(b[0:64, 128:256], ktl[:, cs], arc, True, True, [ktl.tt, AR.tt], [b.tt])
            P.mm(b[0:64, 256:320], AR[:, c, 0, :], btl[:, cs], True, True, [btl.tt, AR.tt], [b.tt])
            P.tt("dve", amat[:, c, :], b[0:64, 0:320], cst[0:64, C_M320:C_M320 + 320], ALU.mult,
                 [b.tt, cst.tt], [amat.tt])
            yield
        P.tt("dve", TTm[0][:, :, :], amat[:, :, 0:64], id64.unsqueeze(1).to_broadcast([64, NCH, CH]), ALU.add,
             [amat.tt, cbf.tt], [TTm[0].tt])
        for lv in range(1, 6):
            dst = XX[lv % 2]
            for hh in range(2):
                b = B_RW
                for cl in range(4):
                    c = hh * 4 + cl
                    if lv == 1:
                        X_, XT_, rd = amat[:, c, 256:320], amat[:, c, 0:64], amat.tt
                    else:
                        X_, XT_, rd = XX[(lv - 1) % 2][:, c, 0:64], XX[(lv - 1) % 2][:, c, 64:128], XX[(lv - 1) % 2].tt
                    P.mm(b[0:64, cl * 128:cl * 128 + 64], XT_, X_, True, True, [rd], [b.tt])
                    P.mm(b[0:64, cl * 128 + 64:cl * 128 + 128], X_, XT_, True, True, [rd], [b.tt])
                P.cp("act", dst[:, hh * 4:hh * 4 + 4, :].rearrange("p c t -> p (c t)"), b[0:64, :], [b.tt], [dst.tt])
                yield
            b = B_RW
            told, tnew = TTm[(lv - 1) % 2], TTm[lv % 2]
            for c in range(NCH):
                P.mm(b[0:64, c * CH:(c + 1) * CH], dst[:, c, 0:64], told[:, c, :], True, True,
                     [dst.tt, told.tt], [b.tt])
            P.tt("dve", tnew[:, :, :], c3(b[0:64, :]), told[:, :, :], ALU.add, [b.tt, told.tt], [tnew.tt])
            yield
        Tf = TTm[5 % 2]
        xx0 = XX[0][:, :, :].rearrange("p c t -> p (c t)")
        xx1 = XX[1][:, :, :].rearrange("p c t -> p (c t)")
        G2bf, W1T, U2 = xx0[:, 0:TS], xx0[:, TS:2 * TS], xx1[:, 0:TS]
        b = B_RW
        for c in range(NCH):
            P.mm(b[0:64, c * CH:(c + 1) * CH], amat[:, c, 128:192], tok[:, c, 2, :], True, True,
                 [amat.tt, tok.tt], [b.tt])
        P.cp("act", G2bf, b[0:64, :], [b.tt, Tf.tt], [XX[0].tt])
        yield
        for c in range(NCH):
            P.mm(b[0:64, c * CH:(c + 1) * CH], tok[:, c, 3, :], Tf[:, c, :], True, True, [tok.tt, Tf.tt], [b.tt])
        P.cp("act", W1T, b[0:64, :], [b.tt], [XX[0].tt])
        yield
        for c in range(NCH):
            P.mm(b[0:64, c * CH:(c + 1) * CH], Tf[:, c, :], G2bf[:, c * CH:(c + 1) * CH], True, True,
                 [Tf.tt, XX[0].tt], [b.tt])
        P.cp("act", U2, b[0:64, :], [b.tt, XX[1].tt], [XX[1].tt])
        yield
        PS_C = B_RW
        for c in range(NCH):
            cs = slice(c * CH, (c + 1) * CH)
            vt = tok[:, c, 2, :]
            P.mm(PS_C[0:64, 64:128], W1T[:, cs], Mbf[:, :], True, True, [XX[0].tt, Mbf.tt], [PS_C.tt])
            P.tt("dve", Ubf[:, :], PS_C[0:64, 64:128], U2[:, cs], ALU.add, [PS_C.tt, XX[1].tt], [Ubf.tt])
            yield
            P.mm(PS_C[0:64, 128:192], tok[:, c, 0, :], Ubf[:, :], True, False, [tok.tt, Ubf.tt], [PS_C.tt])
            P.mm(PS_C[0:64, 128:192], tok[:, c, 1, :], vt, False, True, [tok.tt], [PS_C.tt])
            P.mm(PS_C[0:64, 192:256], Mbf[:, :], AR[:, c, 1, :], True, False, [Mbf.tt, AR.tt], [PS_C.tt])
            P.mm(PS_C[0:64, 192:256], Ubf[:, :], amat[:, c, 64:128], False, False, [Ubf.tt, amat.tt], [PS_C.tt])
            P.mm(PS_C[0:64, 192:256], vt, amat[:, c, 192:256], False, True, [tok.tt, amat.tt], [PS_C.tt])
            P.stt(Mst[:, :], Mst[:, :], pc[:, c:c + 1], PS_C[0:64, 128:192], ALU.mult, ALU.add,
                  [Mst.tt, pc.tt, PS_C.tt], [Mst.tt])
            P.cp("dve", Mbf[:, :], Mst[:, :], [Mst.tt], [Mbf.tt])
            P.cp("act", yrw[:, cs], PS_C[0:64, 192:256], [PS_C.tt], [yrw.tt])
            yield
        while flags is not None and not flags["lru"]:
            yield
        b = B_RW
        P.mm(b[0:64, :], onesf[0:64, 0:64], yrw[:, :], True, True, [cst.tt, yrw.tt], [b.tt])
        P.stt(t1[:, :], b[0:64, :], -1.0 / 64, yrw[:, :], ALU.mult, ALU.add, [b.tt, yrw.tt], [t1.tt])
        P.tt("pool", t2[:, :], t1[:, :], t1[:, :], ALU.mult, [t1.tt], [t2.tt])
        yield
        P.mm(b[0:64, :], onesf[0:64, 0:64], t2[:, :], True, True, [cst.tt, t2.tt], [b.tt])
        P.act(e1[:, :], b[0:64, :], AF.Ln, [b.tt], [e1.tt], bias=GN_EPS, scale=1.0 / 64)
        P.act(e1[:, :], e1[:, :], AF.Exp, [e1.tt], [e1.tt], scale=-0.5)
        P.tt("dve", t1[:, :], t1[:, :], e1[:, :], ALU.mult, [t1.tt, e1.tt], [t1.tt])
        P.ts("dve", t1[:, :], t1[:, :], vcol(V_LNG), ALU.mult, [t1.tt, vec.tt], [t1.tt], vcol(V_LNB), ALU.add)
        P.tt("dve", t1[:, :], t1[:, :], bonus[:, :], ALU.add, [t1.tt, bonus.tt], [t1.tt])
        P.tt("dve", y23[0:64, :], t1[:, :], sgr[:, :], ALU.mult, [t1.tt, sgr.tt], [y23.tt])

    def lru_pre(i, lbank=1):
        P.ts("dve", xc[:, :], lxb[:, 0:TS], vcol(V_CW), ALU.mult, [lxb.tt, vec.tt], [xc.tt], vcol(V_CB), ALU.add)
        for j in range(1, 4):
            P.stt(xc[:, :], lxb[:, j:j + TS], vcol(V_CW + j), xc[:, :], ALU.mult, ALU.add,
                  [lxb.tt, vec.tt, xc.tt], [xc.tt])
        P.cp("pool", lxb[:, 0:3], lxb[:, TS:TS + 3], [lxb.tt], [lxb.tt])
        P.cp("dve", xcb[:, :], xc[:, :], [xc.tt], [xcb.tt])
        yield
        b = ps[lbank]
        P.mm(b[0:64, :], wsmb[:, 128:192], xcb[:, :], True, True, [wsmb.tt, xcb.tt], [b.tt])
        sigmoid_exp(lr[:, :], lr.tt, b[0:64, :], b.tt, xcol(X_NBA))
        yield
        P.mm(b[0:64, :], wsmb[:, 192:256], xcb[:, :], True, True, [wsmb.tt, xcb.tt], [b.tt])
        sigmoid_exp(li[:, :], li.tt, b[0:64, :], b.tt, xcol(X_NBX))
        yield

    def lru_stream(i):
        P.act(la[:, :], lr[:, :], AF.Exp, [lr.tt, vx.tt], [la.tt], scale=xcol(X_CL))
        P.act(lu[:, :], lr[:, :], AF.Exp, [lr.tt, vx.tt], [lu.tt], scale=xcol(X_2CL))
        P.ts("dve", lu[:, :], lu[:, :], -1.0, ALU.mult, [lu.tt], [lu.tt], 1.0, ALU.add)
        P.ts("dve", lu[:, :], lu[:, :], 1e-30, ALU.max, [lu.tt], [lu.tt])
        yield
        P.act(lu[:, :], lu[:, :], AF.Ln, [lu.tt], [lu.tt])
        P.act(lu[:, :], lu[:, :], AF.Exp, [lu.tt], [lu.tt], scale=0.5)
        P.tt("dve", lu[:, :], lu[:, :], li[:, :], ALU.mult, [lu.tt, li.tt], [lu.tt])
        P.tt("dve", lu[:, :], lu[:, :], xc[:, :], ALU.mult, [lu.tt, xc.tt], [lu.tt])
        yield
        P.scan(lh[:, :], la[:, :], lu[:, :], lcar[:, 0:1], ALU.mult, ALU.add, [la.tt, lu.tt, lcar.tt], [lh.tt])
        P.cp("dve", lcar[:, 0:1], lh[:, TS - 1:TS], [lh.tt], [lcar.tt])
        P.tt("dve", y23[64:128, :], lh[:, :], sgl[:, :], ALU.mult, [lh.tt, sgl.tt], [y23.tt])

    def attn_stream(i):
        nkb = 4 * (i + 1)
        m128 = cbf[:, C_M128:C_M128 + 128]
        B_FS = (ps[2], ps[3])
        F_O = ps[4]
        B_SS, PS_R, S_O = ps[0], ps[5], ps[6]
        order = list(range(nkb - 1, -1, -1))

        def fgeom(j):
            d = j - 4 * i
            return d, (0 if d < 0 else 128 * d)

        def fox_scores(j):
            d, col0 = fgeom(j)
            s_ = B_FS[j % 2]
            kc = slice(j * 128, (j + 1) * 128)
            P.mm(s_[:, col0:TS], kaug[:, kc], qaug[:, col0:TS], True, d < 0, [kaug_tt[j // 4], qaug.tt], [s_.tt])
            if d >= 0:
                P.mm(s_[:, col0:col0 + 128], ident, cbf[:, C_FOXM:C_FOXM + 128], False, True, [cbf.tt], [s_.tt])

        def fox_exp(j):
            d, col0 = fgeom(j)
            P.act(pT[j % 2][:, col0:TS], B_FS[j % 2][:, col0:TS], AF.Exp, [B_FS[j % 2].tt], [pT[j % 2].tt])

        def fox_pv(j):
            d, col0 = fgeom(j)
            pt = pT[j % 2]
            P.mm(F_O[0:65, col0:TS], vfox[:, j, :], pt[:, col0:TS], j == 0, j == nkb - 1,
                 [vfox_tt[j // 4], pt.tt], [F_O.tt])

        def geom(t):
            j = order[t]
            d = j - 4 * i
            col0 = 0 if d < 0 else 128 * d
            cr = 0 if t == 0 else col0
            return j, d, col0, cr

        def sb_s(t):
            j, d, col0, cr = geom(t)
            kc = slice(j * 128, (j + 1) * 128)
            P.mm(B_SS[:, col0:TS], ksb[:, kc], qsb[:, col0:TS], True, True, [ksb_tt[j // 4], qsb.tt], [B_SS.tt])

        def sb_e(t):
            j, d, col0, cr = geom(t)
            e_, sp_ = ebuf3[t % 3], spb[t % 2]
            P.act(e_[:, col0:TS], B_SS[:, col0:TS], AF.Exp, [B_SS.tt], [e_.tt])
            P.act(sp_[:, col0:TS], e_[:, col0:TS], AF.Ln, [e_.tt], [sp_.tt], bias=1.0)
            if d >= 0:
                P.tt("pool", sp_[:, col0:col0 + 128], sp_[:, col0:col0 + 128], m128, ALU.mult, [sp_.tt, cbf.tt], [sp_.tt])
            if t == 0 and col0 > 0:
                P.memset("pool", sp_[:, 0:col0], 0.0, [sp_.tt])

        def sb_ru(t):
            j, d, col0, cr = geom(t)
            sp_ = spb[t % 2]
            P.mm(PS_R[:, cr:TS], cbf[:, C_NUINC:C_NUINC + 128], sp_[:, cr:TS], t == 0, False, [cbf.tt, sp_.tt], [PS_R.tt])

        def sb_g(t):
            j, d, col0, cr = geom(t)
            g_ = gbuf[t % 2]
            P.act(g_[:, col0:TS], PS_R[:, col0:TS], AF.Exp, [PS_R.tt], [g_.tt])

        def sb_rl(t):
            j, d, col0, cr = geom(t)
            sp_ = spb[t % 2]
            P.mm(PS_R[:, cr:TS], cbf[:, C_NLOW:C_NLOW + 128], sp_[:, cr:TS], False, t == nkb - 1, [cbf.tt, sp_.tt], [PS_R.tt])

        def sb_a(t):
            j, d, col0, cr = geom(t)
            e_, g_, a_ = ebuf3[t % 3], gbuf[t % 2], abuf[t % 2]
            if t == 0 and col0 > 0:
                P.memset("pool", a_[:, 0:col0], 0.0, [a_.tt])
            P.tt("pool", a_[:, col0:TS], e_[:, col0:TS], g_[:, col0:TS], ALU.mult, [e_.tt, g_.tt], [a_.tt])
            if d >= 0:
                P.tt("pool", a_[:, col0:col0 + 128], a_[:, col0:col0 + 128], m128, ALU.mult, [a_.tt, cbf.tt], [a_.tt])

        def sb_pv(t):
            j, d, col0, cr = geom(t)
            a_ = abuf[t % 2]
            P.mm(S_O[0:64, cr:TS], vsb[:, j, 0:64], a_[:, cr:TS], t == 0, t == nkb - 1, [vsb_tt[j // 4], a_.tt], [S_O.tt])

        fox_scores(0)
        for t in range(nkb + 2):
            if 0 <= t - 1 < nkb:
                sb_ru(t - 1)
                sb_g(t - 1)
            if t < nkb:
                sb_s(t)
            if t + 1 < nkb:
                fox_scores(t + 1)
            if t < nkb:
                fox_exp(t)
                sb_e(t)
            if 0 <= t - 1 < nkb:
                sb_rl(t - 1)
            if 0 <= t - 2 < nkb:
                sb_a(t - 2)
                sb_pv(t - 2)
            if t < nkb:
                fox_pv(t)
            yield
        P.tt("dve", y01[64:128, :], S_O[0:64, :], sgs[:, :], ALU.mult, [S_O.tt, sgs.tt], [y01.tt])
        P.recip(rrow[64:65, :], F_O[64:65, :], [F_O.tt], [rrow.tt])
        bb = B_FS[0]
        P.mm(bb[0:64, :], onesf[64:65, 0:64], rrow[64:65, :], True, True, [cst.tt, rrow.tt], [bb.tt])
        P.cp("act", rcp[:, :], bb[0:64, :], [bb.tt], [rcp.tt])
        P.tt("dve", rcp[:, :], F_O[0:64, :], rcp[:, :], ALU.mult, [F_O.tt, rcp.tt], [rcp.tt])
        P.tt("dve", y01[0:64, :], rcp[:, :], sgf[:, :], ALU.mult, [rcp.tt, sgf.tt], [y01.tt])

    def run_streams(streams):
        alive = [[g, w] for g, w in streams]
        while alive:
            for ent in list(alive):
                for _ in range(ent[1]):
                    try:
                        next(ent[0])
                    except StopIteration:
                        alive.remove(ent)
                        break

    def reset_state():
        P.memset("dve", fcar[:, :], 0.0, [fcar.tt])
        for b_ in (rbuf, kbuf, vbuf, wabuf):
            P.memset("pool", b_[:, 0:1], 0.0, [b_.tt])
        P.memset("dve", Mst[:, :], 0.0, [Mst.tt])
        P.memset("dve", Mbf[:, :], 0.0, [Mbf.tt])
        P.memset("pool", lxb[:, 0:3], 0.0, [lxb.tt])
        P.memset("dve", lcar[:, :], 0.0, [lcar.tt])

    def front_a1(l, i):
        xsq_, rstd_ = (FB, FB2), FS
        if l > 0:
            P.dma(ypb[:, :, :], yg_tile(l - 1, i), [yg_tt[l - 1][i // CQ]], [ypb.tt])
            for m in range(8):
                b = ps[1]
                for k in range(8):
                    P.mm(b[:, :], woutb[:, k, m * 128:(m + 1) * 128], ypb[:, k, :], k == 0, k == 7,
                         [woutb.tt, ypb.tt], [b.tt])
                P.tt("dve", xt[:, m, :], xt[:, m, :], b[:, :], ALU.add, [xt.tt, b.tt], [xt.tt])
                yield
        b = ps[1]
        for k in range(8):
            xq = xsq_[k % 2]
            P.tt("pool", xq[:, :], xt[:, k, :], xt[:, k, :], ALU.mult, [xt.tt], [xq.tt])
            P.mm(b[:, :], onesb, xq[:, :], k == 0, k == 7, [cbf.tt, xq.tt], [b.tt])
        P.act(rstd_[:, :], b[:, :], AF.Ln, [b.tt], [rstd_.tt], bias=RMS_EPS, scale=1.0 / D)
        P.act(rstd_[:, :], rstd_[:, :], AF.Exp, [rstd_.tt], [rstd_.tt], scale=-0.5)
        yield
        for k in range(8):
            P.tt("dve" if k % 2 == 0 else "pool", xn[:, k, :], xt[:, k, :], rstd_[:, :], ALU.mult,
                 [xt.tt, rstd_.tt], [xn.tt])
            if k % 4 == 3:
                yield
        if i + 1 < NT:
            load_x(i + 1)

    def front_a2(l, i):
        c0 = i * TS
        b = proj_fm(CG_QQ, 128, "pa")
        P.ts("dve", qaug[0:64, :], b[0:64, :], 0.125, ALU.mult, [b.tt], [qaug.tt])
        P.ts("dve", qsb[0:64, :], b[64:128, :], 0.125, ALU.mult, [b.tt], [qsb.tt])
        yield
        b = proj_fm(CG_KK, 128, "pa")
        P.cp("dve", kaug[0:64, c0:c0 + TS], b[0:64, :], [b.tt], [kaug_tt[i]])
        P.cp("act", ksb[0:64, c0:c0 + TS], b[64:128, :], [b.tt], [ksb_tt[i]])
        yield
        b = proj_fm(CG_GA, 128, "pa")
        sigmoid_exp(FS[:, :], FS.tt, b[:, :], b.tt, 0.0)
        P.tt("dve", sgf[:, :], b[0:64, :], FS[0:64, :], ALU.mult, [b.tt, FS.tt], [sgf.tt])
        P.tt("dve", sgs[:, :], b[64:128, :], FS[64:128, :], ALU.mult, [b.tt, FS.tt], [sgs.tt])
        yield
        b = proj_fm(CG_FF, 1, "pa")
        P.act(FS[0:1, :], b[0:1, :], AF.Exp, [b.tt, vx.tt], [FS.tt], bias=xcol(X_NBF, 1), scale=-1.0)
        P.act(FS[0:1, :], FS[0:1, :], AF.Ln, [FS.tt], [FS.tt], bias=1.0)
        P.scan(FS[32:33, :], cst[0:1, C_ONE5:C_ONE5 + TS], FS[0:1, :], fcar[:, 0:1], ALU.mult, ALU.add,
               [cst.tt, FS.tt, fcar.tt], [FS.tt])
        P.cp("dve", fcar[:, 0:1], FS[32:33, TS - 1:TS], [FS.tt], [fcar.tt])
        P.cp("dve", FB[32:33, :], FS[32:33, :], [FS.tt], [FB.tt])
        P.tt("dve", FS[64:65, :], FS[32:33, :], FB[32:33, :], ALU.subtract, [FS.tt, FB.tt], [FS.tt])
        P.cp("dve", FB[64:65, :], FS[64:65, :], [FS.tt], [FB.tt])
        P.tt("dve", FS[0:1, :], FS[64:65, :], FB[64:65, :], ALU.subtract, [FS.tt, FB.tt], [FS.tt])
        P.cp("dve", FB[0:1, :], FS[0:1, :], [FS.tt], [FB.tt])
        P.ts("dve", FB2[:, :], FB[:, :], -1.0, ALU.mult, [FB.tt], [FB2.tt])
        for r3, p_ in enumerate((32, 64, 0)):
            P.dma(kaug[67 + r3:68 + r3, c0:c0 + TS], FB[p_:p_ + 1, :], [FB.tt], [kaug_tt[i]])
            P.dma(qaug[64 + r3:65 + r3, :], FB2[p_:p_ + 1, :], [FB2.tt], [qaug.tt])
        yield
        b = bank("pa")
        for s4 in range(4):
            for k in range(8):
                P.mm(b[:, s4 * 128:(s4 + 1) * 128], xn[:, k, s4 * 128:(s4 + 1) * 128],
                     wbf[:, k, CG_VV:CG_VV + 128], k == 0, k == 7, [wbf.tt, xn.tt], [b.tt])
        bv = b[:, :].rearrange("p (s c) -> p s c", s=4)
        P.cp("dve", vfox[:, 4 * i:4 * i + 4, 0:64], bv[:, :, 0:64], [b.tt], [vfox_tt[i]])
        P.cp("act", vsb[:, 4 * i:4 * i + 4, 0:64], bv[:, :, 64:128], [b.tt], [vsb_tt[i]])

    def front_b(l, i):
        b = proj_fm(CG_GB, 128)
        sigmoid_exp(FS[:, :], FS.tt, b[:, :], b.tt, 0.0)
        P.tt("dve", sgr[:, :], b[0:64, :], FS[0:64, :], ALU.mult, [b.tt, FS.tt], [sgr.tt])
        P.tt("dve", sgl[:, :], b[64:128, :], FS[64:128, :], ALU.mult, [b.tt, FS.tt], [sgl.tt])
        b = proj_fm(CG_RK, 128)
        P.cp("dve", rbuf[:, 1:TS + 1], b[0:64, :], [b.tt], [rbuf.tt])
        P.cp("act", kbuf[:, 1:TS + 1], b[64:128, :], [b.tt], [kbuf.tt])
        b = proj_fm(CG_VWA, 128)
        P.cp("dve", vbuf[:, 1:TS + 1], b[0:64, :], [b.tt], [vbuf.tt])
        P.cp("act", wabuf[:, 1:TS + 1], b[64:128, :], [b.tt], [wabuf.tt])
        b = proj_fm(CG_LX, 64)
        P.cp("dve", lxb[:, 3:TS + 3], b[0:64, :], [b.tt], [lxb.tt])

    def run_layer(l):
        load_small(l)
        if l == 0:
            for _ in load_win(0):
                pass
        reset_state()
        load_x(0)
        for _ in front_a1(l, 0):
            pass
        for _ in front_a2(l, 0):
            pass
        for i in range(NT):
            c0 = i * TS
            front_b(l, i)

            flags = {"pre": False, "lru": False}
            lru_in_attn = (4 * (i + 1) + 2) < 40

            def attn_all(i=i, flags=flags, lru_in_attn=lru_in_attn):
                yield from attn_stream(i)
                if lru_in_attn:
                    while not flags["pre"]:
                        yield
                    yield from lru_pre(i, 0)
                    yield from lru_stream(i)
                    flags["lru"] = True
                if i + 1 < NT:
                    while not flags.get("fa1"):
                        yield
                    yield from front_a2(l, i + 1)

            def rw_all(i=i, flags=flags, lru_in_attn=lru_in_attn):
                yield from rwkv_tile(i)
                flags["pre"] = True
                if not lru_in_attn:
                    yield from lru_pre(i)
                    yield from lru_stream(i)
                    flags["lru"] = True
                yield from rwkv_stream(i, flags)
            def fa1_all(i=i, flags=flags):
                yield from front_a1(l, i + 1)
                flags["fa1"] = True
            strs = [(attn_all(), 1), (rw_all(), 1)]
            if i + 1 < NT:
                strs.append((fa1_all(), 1))
            else:
                strs.append((prefetch_next(l), 1))
            run_streams(strs)
            q_, o_ = i // CQ, (i % CQ) * TS
            P.dma(yown_ds[l][q_][0:128, o_:o_ + TS], y01[:, :], [y01.tt], [yown_tt[l][i][0]])
            P.dma(yown_ds[l][q_][128:256, o_:o_ + TS], y23[:, :], [y23.tt], [yown_tt[l][i][1]])
            if i % CQ == CQ - 1:
                rd = [t for ii in range(q_ * CQ, (q_ + 1) * CQ) for t in yown_tt[l][ii]]
                grp = [list(g) for g in groups]
                P.op("pool", (lambda l_, q2: (lambda e: e.collective_compute(
                    "AllGather", ALU.bypass, replica_groups=grp, ins=[yown_ds[l_][q2]], outs=[yg_ds[l_][q2]])))(l, q_),
                    rd, [yg_tt[l][q_]], cc=True)

    def run_final():
        wf = [woutb, wbf]
        yb = [ypb, xn]
        x2_tt = [TT("x2_%d" % i) for i in range(NT)]
        ssq_tt = [TT("ssq_%d" % i) for i in range(NT)]
        ssqr_tt = TT("ssqr")
        for i in range(NT):
            c0 = i * TS
            for l in range(nlayers):
                P.dma(yb[l][:, :, :], yg_tile(l, i), [yg_tt[l][i // CQ]], [yb[l].tt])
            P.dma(xt[:, 0:2, :], xfin_d[:, c0:c0 + TS].rearrange("(j p) t -> p j t", p=128), [], [xt.tt])
            bq = bank("sc")
            for j in range(2):
                b = bank("proj")
                n = 0
                for l in range(nlayers):
                    for k in range(8):
                        P.mm(b[:, :], wf[l][:, k, j * 128:(j + 1) * 128], yb[l][:, k, :], n == 0,
                             n == 8 * nlayers - 1, [wf[l].tt, yb[l].tt], [b.tt])
                        n += 1
                P.tt("dve", xt[:, j, :], xt[:, j, :], b[:, :], ALU.add, [xt.tt, b.tt], [xt.tt])
                xq = xsq[j]
                P.tt("pool", xq[:, :], xt[:, j, :], xt[:, j, :], ALU.mult, [xt.tt], [xq.tt])
                P.mm(bq[0:1, :], onesb[:, 0:1], xq[:, :], j == 0, j == 1, [cbf.tt, xq.tt], [bq.tt])
            P.cp("dve", t1[0:1, :], bq[0:1, :], [bq.tt], [t1.tt])
            P.dma(ssq_d[0:1, c0:c0 + TS], t1[0:1, :], [t1.tt], [ssq_tt[i]])
            P.dma(x2_d[:, c0:c0 + TS].rearrange("(j p) t -> p j t", p=128), xt[:, 0:2, :], [xt.tt], [x2_tt[i]])
        grp = [list(g) for g in groups]
        P.op("pool", lambda e: e.collective_compute("AllReduce", ALU.add, replica_groups=grp,
                                                    ins=[ssq_d], outs=[ssqr_d]),
             ssq_tt, [ssqr_tt], cc=True)
        for i in range(NT):
            c0 = i * TS
            sl = 2 * (i % 4)
            P.dma(xt[:, sl:sl + 2, :], x2_d[:, c0:c0 + TS].rearrange("(j p) t -> p j t", p=128), [x2_tt[i]], [xt.tt])
            rs_ = fscr
            P.dma(rs_[:, :], ssqr_d[0:1, c0:c0 + TS].partition_broadcast(128), [ssqr_tt], [rs_.tt])
            P.act(rs_[:, :], rs_[:, :], AF.Ln, [rs_.tt], [rs_.tt], bias=RMS_EPS, scale=1.0 / D)
            P.act(rs_[:, :], rs_[:, :], AF.Exp, [rs_.tt], [rs_.tt], scale=-0.5)
            for j in range(2):
                P.stt(xt[:, sl + j, :], xt[:, sl + j, :], vec[:, V_FG + j:V_FG + j + 1], rs_[:, :], ALU.mult, ALU.mult,
                      [xt.tt, vec.tt, rs_.tt], [xt.tt])
            P.dma(out_d[:, c0:c0 + TS].rearrange("(j p) t -> p j t", p=128), xt[:, sl:sl + 2, :], [xt.tt], [])

    for l in range(nlayers):
        run_layer(l)
    run_final()
    P.finish()
    P.emit()
    es.close()
    P.sb_used = sb_used[0]
    return nc, P


def perm_wout(w):
    return np.ascontiguousarray(w.reshape(4, 4, 64, D).transpose(1, 0, 2, 3).reshape(D, D))


def make_maps(inp, B, nlayers=DEPTH):
    consts = make_consts()
    x = inp["x"]
    xT = [np.ascontiguousarray(x[b].T) for b in range(B)]
    wp = [perm_wout(inp["w_out"][l]) for l in range(nlayers)]
    maps = []
    for c in range(4 * B):
        b, h = divmod(c, 4)
        m = {"xT": xT[b], "consts": consts, "xfin": np.ascontiguousarray(xT[b][256 * h:256 * h + 256, :])}
        for l in range(nlayers):
            p = pack_layer_inputs(inp, l, h)
            m["win%d" % l] = p["win"]
            m["vecs%d" % l] = p["vecs"]
            m["wsm%d" % l] = p["wsm"]
            m["woutp%d" % l] = wp[l]
            m["woutf%d" % l] = np.ascontiguousarray(wp[l][:, 256 * h:256 * h + 256])
        maps.append(m)
    return maps


def kernel(**inputs):
    inp = {k: np.asarray(v, dtype=np.float32) for k, v in inputs.items()}
    B, S, _ = inp["x"].shape
    nc, _ = build_fused(S)
    maps = make_maps(inp, B)
    res = run_bass_kernel_spmd(nc, maps, core_ids=list(range(4 * B)))
    out = np.empty((B, S, D), np.float32)
    for c in range(4 * B):
        b, h = divmod(c, 4)
        out[b, :, 256 * h:256 * h + 256] = res.results[c]["outT"].T
    return out
```

```python
import numpy as np
from contextlib import ExitStack
import concourse.bass as bass
import concourse.mybir as mybir
from concourse.bass_utils import run_bass_kernel_spmd

F32 = mybir.dt.float32
BF16 = mybir.dt.bfloat16
AF = mybir.ActivationFunctionType
ALU = mybir.AluOpType

D = 1024
G = 256
HD = 64
R = 32
DEPTH = 2
BATCH = 2
SEQ = 8192
N_IN = 3652
TS = 512
CH = 64
NCH = TS // CH
RMS_EPS = 1e-6
GN_EPS = 64e-5
LRU_C = 8.0
NEG = -30000.0


class TT:
    __slots__ = ("name", "last_w", "readers")

    def __init__(self, name):
        self.name = name
        self.last_w = None
        self.readers = {}


class Prog:
    ENG = ("pe", "act", "dve", "pool", "sp")
    NDS = 24

    def __init__(self, nc, es):
        self.nc = nc
        self.es = es
        self.semh = {}
        for e in self.ENG:
            self.semh["e:" + e] = es.enter_context(nc.semaphore("sem_" + e))
        for i in range(self.NDS):
            self.semh["d:%d" % i] = es.enter_context(nc.semaphore("dsem%d" % i))
        self.cnt = {e: 0 for e in self.ENG}
        self.ops = {e: [] for e in self.ENG}
        self.seen = {e: {} for e in self.ENG}
        self.dma_cnt = [0] * self.NDS
        self.dma_rr = 0
        self.nops = 0
        self.semh["cc"] = es.enter_context(nc.semaphore("sem_cc"))
        self.cc_cnt = 0

    def op(self, eng, fn, reads=(), writes=(), dma=False, cc=False):
        waits = {}

        def need(ev, same_ok):
            if ev is None:
                return
            skey, val, src = ev
            if same_ok and src == eng:
                return
            if waits.get(skey, 0) < val:
                waits[skey] = val

        for t in reads:
            need(t.last_w, eng == "pe")
        for t in writes:
            need(t.last_w, True)
            for skey, (val, src) in t.readers.items():
                need((skey, val, src), True)
        if dma:
            i = self.dma_rr
            self.dma_rr = (i + 1) % self.NDS
            k = self.dma_cnt[i]
            if k > 0:
                need(("d:%d" % i, 16 * k, "dma"), False)
            self.dma_cnt[i] = k + 1
            ev = ("d:%d" % i, 16 * (k + 1), "dma")
            inc = 16
        elif cc:
            self.cc_cnt += 1
            ev = ("cc", self.cc_cnt, "cc")
            inc = None
        else:
            self.cnt[eng] += 1
            ev = ("e:" + eng, self.cnt[eng], eng)
            inc = 1
        seen = self.seen[eng]
        wl = []
        for skey, val in waits.items():
            if seen.get(skey, 0) < val:
                seen[skey] = val
                wl.append((skey, val))
        self.ops[eng].append((wl, fn, ev[0], inc))
        self.nops += 1
        for t in reads:
            old = t.readers.get(ev[0])
            if old is None or old[0] < ev[1]:
                t.readers[ev[0]] = (ev[1], ev[2])
        for t in writes:
            t.last_w = ev
            t.readers = {}
        return ev

    def finish(self):
        wl = []
        for i in range(self.NDS):
            if self.dma_cnt[i] > 0:
                wl.append(("d:%d" % i, 16 * self.dma_cnt[i]))
        for e in self.ENG:
            if e != "sp" and self.cnt[e] > 0:
                wl.append(("e:" + e, self.cnt[e]))
        if self.cc_cnt > 0:
            wl.append(("cc", self.cc_cnt))
        self.ops["sp"].append((wl, None, None, 0))

    def emit(self):
        nc = self.nc
        with nc.Block() as block:
            def mk(e):
                def body(engh):
                    for wl, fn, skey, inc in self.ops[e]:
                        for k, v in wl:
                            engh.wait_ge(self.semh[k], v)
                        if fn is not None:
                            ins = fn(engh)
                            if inc is None:
                                ins.then_inc(self.semh[skey])
                            else:
                                ins.then_inc(self.semh[skey], inc)
                return body
            block.tensor(mk("pe"))
            block.scalar(mk("act"))
            block.vector(mk("dve"))
            block.gpsimd(mk("pool"))
            block.sync(mk("sp"))

    def mm(self, out, lhsT, rhs, start, stop, r, w):
        return self.op("pe", lambda e: e.matmul(out, lhsT, rhs, start=start, stop=stop), r, w)

    def tr(self, out, in_, ident, r, w):
        return self.op("pe", lambda e: e.transpose(out, in_, ident), r, w)

    def act(self, out, in_, func, r, w, bias=0.0, scale=1.0):
        return self.op("act", lambda e: e.activation(out=out, in_=in_, func=func, bias=bias, scale=scale), r, w)

    def tt(self, eng, out, in0, in1, op, r, w):
        return self.op(eng, lambda e: e.tensor_tensor(out=out, in0=in0, in1=in1, op=op), r, w)

    def ts(self, eng, out, in0, s1, op0, r, w, s2=None, op1=None):
        if op1 is None:
            return self.op(eng, lambda e: e.tensor_scalar(out=out, in0=in0, scalar1=s1, scalar2=None, op0=op0), r, w)
        return self.op(eng, lambda e: e.tensor_scalar(out=out, in0=in0, scalar1=s1, scalar2=s2, op0=op0, op1=op1), r, w)

    def stt(self, out, in0, scalar, in1, op0, op1, r, w):
        return self.op("dve", lambda e: e.scalar_tensor_tensor(out=out, in0=in0, scalar=scalar, in1=in1, op0=op0, op1=op1), r, w)

    def cp(self, eng, out, in_, r, w):
        if eng == "act":
            return self.op("act", lambda e: e.copy(out=out, in_=in_), r, w)
        return self.op(eng, lambda e: e.tensor_copy(out=out, in_=in_), r, w)

    def memset(self, eng, ap, val, w):
        return self.op(eng, lambda e: e.memset(ap, val), (), w)

    def scan(self, out, d0, d1, init, op0, op1, r, w):
        return self.op("dve", lambda e: e.tensor_tensor_scan(out=out, data0=d0, data1=d1, initial=init, op0=op0, op1=op1), r, w)

    def recip(self, out, in_, r, w):
        return self.op("dve", lambda e: e.reciprocal(out=out, in_=in_), r, w)

    def dma(self, out, in_, r, w, eng="sp"):
        return self.op(eng, lambda e: e.dma_start(out=out, in_=in_), r, w, dma=True)


class Buf:
    def __init__(self, t, name):
        self.t = t
        self.tt = TT(name)

    def __getitem__(self, k):
        return self.t[k]


C_ID = 0
C_FOXM = C_ID + 128
C_M128 = C_FOXM + 128
C_NUINC = C_M128 + 128
C_NLOW = C_NUINC + 128
C_ONES = C_NLOW + 128
C_M320 = C_ONES + 128
C_RST = C_M320 + 320
C_ONE5 = C_RST + 512
NCONST = C_ONE5 + 512


def make_consts():
    c = np.zeros((128, NCONST), np.float32)
    i = np.arange(128)[:, None]
    j = np.arange(128)[None, :]
    c[:, C_ID:C_ID + 128] = (i == j)
    c[:, C_FOXM:C_FOXM + 128] = np.where(i <= j, 0.0, NEG)
    c[:, C_M128:C_M128 + 128] = (j > i)
    c[:, C_NUINC:C_NUINC + 128] = -1.0 * (i >= j)
    c[:, C_NLOW:C_NLOW + 128] = -1.0 * (i < j)
    c[:, C_ONES:C_ONES + 128] = 1.0
    a = np.arange(64)[:, None]
    b = np.arange(64)[None, :]
    m = np.zeros((64, 320), np.float32)
    m[:, 0:64] = (b > a)
    m[:, 64:128] = (b >= a)
    m[:, 128:192] = (b > a)
    m[:, 192:256] = (b >= a)
    m[:, 256:320] = (b < a)
    c[0:64, C_M320:C_M320 + 320] = m
    rst = np.ones((512,), np.float32)
    rst[::64] = 0.0
    c[:, C_RST:C_RST + 512] = rst[None, :]
    c[:, C_ONE5:C_ONE5 + 512] = 1.0
    return c


V_G = 0
V_MU_R = 8
V_MU_K = 9
V_MU_V = 10
V_MU_WA = 11
V_W0 = 12
V_A0 = 13
V_KK = 14
V_KA = 15
V_RK = 16
V_LNG = 17
V_LNB = 18
V_CW = 19
V_CB = 23
V_BA = 24
V_BX = 25
V_LAM = 26
V_BF = 27
V_FG = 28
NVEC = 36

CG_QQ = 0
CG_KK = 128
CG_GA = 256
CG_GB = 384
CG_RK = 512
CG_VWA = 640
CG_LX = 768
CG_FF = 832
CG_VV = 833
NCOL = 961


def col_index(h):
    g = lambda base: list(range(base + 64 * h, base + 64 * h + 64))
    fq, fk, fv, fg = g(0), g(256), g(512), g(768)
    ff = [1024 + h]
    sq, sk, sv, sg = g(1028), g(1284), g(1540), g(1796)
    rr, rk, rv = g(2052), g(2308), g(2564)
    wl = list(range(2820, 2852))
    al = list(range(2852, 2884))
    rg, lx, lg = g(2884), g(3140), g(3396)
    cols = fq + sq + fk + sk + fg + sg + rg + lg + rr + rk + rv + wl + al + lx + ff + fv + sv
    assert len(cols) == NCOL
    return np.array(cols)


def pack_layer_inputs(inp, l, h):
    f = np.float32
    hs = slice(64 * h, 64 * h + 64)
    win = np.ascontiguousarray(inp["w_in"][l][:, col_index(h)]).astype(f)
    vecs = np.zeros((128, NVEC), f)
    vecs[:, V_G:V_G + 8] = inp["norm_g"][l].reshape(8, 128).T
    mu = inp["rwkv_mu"][l]
    vecs[0:64, V_MU_R] = mu[0:256][hs]
    vecs[0:64, V_MU_K] = mu[256:512][hs]
    vecs[0:64, V_MU_V] = mu[512:768][hs]
    vecs[0:32, V_MU_WA] = mu[768:800]
    vecs[32:64, V_MU_WA] = mu[800:832]
    vecs[0:64, V_W0] = inp["rwkv_w0"][l][hs]
    vecs[0:64, V_A0] = inp["rwkv_a0"][l][hs]
    vecs[0:64, V_KK] = inp["rwkv_k_k"][l][hs]
    vecs[0:64, V_KA] = inp["rwkv_k_a"][l][hs]
    vecs[0:64, V_RK] = inp["rwkv_r_k"][l][h]
    vecs[0:64, V_LNG] = inp["rwkv_ln_g"][l][hs]
    vecs[0:64, V_LNB] = inp["rwkv_ln_b"][l][hs]
    for i in range(4):
        vecs[0:64, V_CW + i] = inp["lru_conv_w"][l][i][hs]
    vecs[0:64, V_CB] = inp["lru_conv_b"][l][hs]
    vecs[0:64, V_BA] = inp["lru_b_a"][l][hs]
    vecs[0:64, V_BX] = inp["lru_b_x"][l][hs]
    vecs[0:64, V_LAM] = inp["lru_lambda"][l][hs]
    vecs[0, V_BF] = inp["b_forget"][l][h]
    vecs[:, V_FG:V_FG + 2] = inp["final_g"][256 * h:256 * h + 256].reshape(2, 128).T
    wsm = np.zeros((64, 256), f)
    wsm[0:32, 0:64] = inp["rwkv_w2"][l][:, hs]
    wsm[32:64, 64:128] = inp["rwkv_a2"][l][:, hs]
    wsm[:, 128:192] = inp["lru_w_a"][l][h]
    wsm[:, 192:256] = inp["lru_w_x"][l][h]
    return {"win": win, "vecs": vecs, "wsm": wsm}


def build_fused(S, parts=("rwkv", "lru", "fox", "sb"), nlayers=DEPTH, final=True, groups=((0, 1, 2, 3), (4, 5, 6, 7))):
    NT = S // TS
    NB = S // 128
    nc = bass.Bass("TRN2", target_bir_lowering=False)
    es = ExitStack()
    P = Prog(nc, es)

    def dram(name, shape, dt, kind):
        return nc.dram_tensor(name, shape, dt, kind=kind).ap()

    xT_d = dram("xT", [D, S], F32, "ExternalInput")
    win_ds = [dram("win%d" % l, [D, NCOL], F32, "ExternalInput") for l in range(nlayers)]
    vecs_ds = [dram("vecs%d" % l, [128, NVEC], F32, "ExternalInput") for l in range(nlayers)]
    wsm_ds = [dram("wsm%d" % l, [64, 256], F32, "ExternalInput") for l in range(nlayers)]
    woutp_ds = [dram("woutp%d" % l, [D, D], F32, "ExternalInput") for l in range(nlayers)]
    consts_d = dram("consts", [128, NCONST], F32, "ExternalInput")
    out_d = dram("outT", [256, S], F32, "ExternalOutput")
    xfin_d = dram("xfin", [256, S], F32, "ExternalInput")
    woutf_ds = [dram("woutf%d" % l, [D, 256], F32, "ExternalInput") for l in range(nlayers)]
    x2_d = dram("x2_i", [256, S], F32, "Internal")
    ssq_d = dram("ssq_i", [1, S], F32, "Internal")
    ssqr_d = dram("ssqr_i", [1, S], F32, "Internal")
    CQ = min(4, NT)
    NQ = NT // CQ
    yown_ds = [[dram("yown%d_%d" % (l, q), [256, CQ * TS], BF16, "Internal") for q in range(NQ)] for l in range(nlayers)]
    yg_ds = [[dram("yg%d_%d" % (l, q), [D, CQ * TS], BF16, "Internal") for q in range(NQ)] for l in range(nlayers)]
    yown_tt = [[(TT("yo%d_%d_a" % (l, i)), TT("yo%d_%d_b" % (l, i))) for i in range(NT)] for l in range(nlayers)]
    yg_tt = [[TT("yg%d_%d" % (l, q)) for q in range(NQ)] for l in range(nlayers)]

    def yg_tile(l, i):
        return yg_ds[l][i // CQ][:, (i % CQ) * TS:(i % CQ + 1) * TS].rearrange("(k p) t -> p k t", p=128)
    with_prev = True

    sb_used = [0]

    def sb(name, shape, dt=F32):
        n = 1
        for d_ in shape[1:]:
            n *= d_
        sb_used[0] += ((n * (2 if dt == BF16 else 4) + 31) // 32) * 32
        return Buf(es.enter_context(nc.sbuf_tensor(name, shape, dt)), name)

    def psb(name, shape, dt=F32):
        return Buf(es.enter_context(nc.psum_tensor(name, shape, dt)), name)

    cst = sb("cst", [128, NCONST])
    cbf = sb("cbf", [128, C_M320], BF16)
    vec = sb("vec", [128, NVEC])
    vx = sb("vx", [128, 16])
    wsmb = sb("wsm_b", [64, 256], BF16)
    wbf = sb("wbf", [128, 8, D], BF16)
    woutb = sb("woutb", [128, 8, D], BF16)

    P.dma(cst[:, :], consts_d[:, :], [], [cst.tt])
    P.cp("dve", cbf[:, :], cst[:, 0:C_M320], [cst.tt], [cbf.tt])
    ident = cbf[:, C_ID:C_ID + 128]
    onesb = cbf[:, C_ONES:C_ONES + 128]
    onesf = cst[:, C_ONES:C_ONES + 128]
    X_NW0, X_NA0, X_1MKA, X_CL, X_2CL, X_NBA, X_NBX, X_NBF, X_T = 0, 1, 2, 3, 4, 5, 6, 7, 8

    def load_small(l):
        P.dma(vec[:, :], vecs_ds[l][:, :], [], [vec.tt])
        P.dma(t1[:, 0:256], wsm_ds[l][:, :], [], [t1.tt])
        P.cp("dve", wsmb[:, :], t1[:, 0:256], [t1.tt], [wsmb.tt])
        P.ts("dve", vx[0:64, X_NW0:X_NW0 + 1], vec[0:64, V_W0:V_W0 + 1], -1.0, ALU.mult, [vec.tt], [vx.tt])
        P.ts("dve", vx[0:64, X_1MKA:X_1MKA + 1], vec[0:64, V_KA:V_KA + 1], -1.0, ALU.mult, [vec.tt], [vx.tt], 1.0, ALU.add)
        P.act(vx[0:64, X_T:X_T + 1], vec[0:64, V_LAM:V_LAM + 1], AF.Exp, [vec.tt], [vx.tt], scale=-1.0)
        P.act(vx[0:64, X_T:X_T + 1], vx[0:64, X_T:X_T + 1], AF.Ln, [vx.tt], [vx.tt], bias=1.0)
        P.ts("dve", vx[0:64, X_CL:X_CL + 1], vx[0:64, X_T:X_T + 1], -LRU_C, ALU.mult, [vx.tt], [vx.tt])
        P.ts("dve", vx[0:64, X_2CL:X_2CL + 1], vx[0:64, X_T:X_T + 1], -2.0 * LRU_C, ALU.mult, [vx.tt], [vx.tt])
        P.ts("dve", vx[0:1, X_NBF:X_NBF + 1], vec[0:1, V_BF:V_BF + 1], -1.0, ALU.mult, [vec.tt], [vx.tt])
        P.ts("dve", vx[0:64, X_NA0:X_NA0 + 1], vec[0:64, V_A0:V_A0 + 1], -1.0, ALU.mult, [vec.tt], [vx.tt])
        P.ts("dve", vx[0:64, X_NBA:X_NBA + 1], vec[0:64, V_BA:V_BA + 1], -1.0, ALU.mult, [vec.tt], [vx.tt])
        P.ts("dve", vx[0:64, X_NBX:X_NBX + 1], vec[0:64, V_BX:V_BX + 1], -1.0, ALU.mult, [vec.tt], [vx.tt])

    ps = [psb("ps%d" % i, [128, 512]) for i in range(7)]
    psT = psb("psT", [128, 1024], BF16)
    rot = {"proj": [0, 1], "sc": [2, 3], "pa": [0, 2, 3]}
    rot_i = {"proj": 0, "sc": 0, "pa": 0}

    def bank(kind):
        l = rot[kind]
        b = ps[l[rot_i[kind] % len(l)]]
        rot_i[kind] += 1
        return b

    PS_O, PS_R, PS_C = ps[4], ps[5], ps[6]

    kaug = sb("kaug", [128, S], BF16)
    kaug_tt = [TT("kaug%d" % i) for i in range(NT)]
    ksb = sb("ksb", [128, S], BF16)
    ksb_tt = [TT("ksb%d" % i) for i in range(NT)]
    vfox = sb("vfox", [128, NB, 65], BF16)
    vfox_tt = [TT("vfox%d" % i) for i in range(NT)]
    vsb = sb("vsb", [128, NB, 65], BF16)
    vfoxf = vfox[:, :, :].rearrange("p b c -> p (b c)")
    vsbf = vsb[:, :, :].rearrange("p b c -> p (b c)")
    vsb_tt = [TT("vsb%d" % i) for i in range(NT)]
    P.memset("pool", kaug[64:128, :], 0.0, kaug_tt)
    P.memset("pool", kaug[64:70, :], 1.0, kaug_tt)
    P.memset("pool", ksb[64:128, :], 0.0, ksb_tt)
    P.memset("pool", vsb[:, :, 64:65], 0.0, vsb_tt)
    P.memset("pool", vfox[:, :, 64:65], 1.0, vfox_tt)

    xt = sb("xt", [128, 8, TS])
    xtw = xt[:, :, :].rearrange("p k t -> p (k t)")

    gnx = sb("gnx", [128, 8])

    def load_win(l):
        P.dma(gnx[:, :], vecs_ds[l][:, V_G:V_G + 8], [], [gnx.tt])
        for k in range(8):
            st = xtw[:, (k % 4) * 1024:(k % 4) * 1024 + NCOL]
            P.dma(st, win_ds[l][k * 128:(k + 1) * 128, :], [], [xt.tt])
            P.ts("dve", wbf[:, k, 0:NCOL], st, gnx[:, k:k + 1], ALU.mult, [xt.tt, gnx.tt], [wbf.tt])
            yield

    def load_wout(dst, l):
        for k in range(8):
            st = xtw[:, (k % 4) * 1024:(k % 4) * 1024 + D]
            P.dma(st, woutp_ds[l][k * 128:(k + 1) * 128, :], [], [xt.tt])
            P.cp("dve", dst[:, k, :], st, [xt.tt], [dst.tt])
            yield

    def load_woutf():
        for l, dst in enumerate((woutb, wbf)):
            for k in range(8):
                st = xtw[:, (k % 4) * 1024:(k % 4) * 1024 + 256]
                P.dma(st, woutf_ds[l][k * 128:(k + 1) * 128, :], [], [xt.tt])
                P.cp("dve", dst[:, k, 0:256], st, [xt.tt], [dst.tt])
                yield

    def prefetch_next(l):
        if l + 1 < nlayers:
            yield from load_win(l + 1)
            yield from load_wout(woutb, l)
        else:
            yield from load_woutf()

    xn = sb("xn", [128, 8, TS], BF16)
    ypb = sb("ypb", [128, 8, TS], BF16)
    qaug = sb("qaug", [128, TS], BF16)
    P.memset("pool", qaug[64:128, :], 0.0, [qaug.tt])
    P.memset("pool", qaug[64:70, :], 1.0, [qaug.tt])
    qsb = sb("qsb", [128, TS], BF16)
    P.memset("pool", qsb[64:128, :], 0.0, [qsb.tt])
    sgf = sb("sgf", [64, TS], BF16)
    sgs = sb("sgs", [64, TS], BF16)
    sgr = sb("sgr", [64, TS], BF16)
    sgl = sb("sgl", [64, TS], BF16)
    FS = sb("FS", [128, TS])
    FB = sb("FB", [128, TS], BF16)
    FB2 = sb("FB2", [128, TS], BF16)
    P.memset("pool", FS[:, :], 0.0, [FS.tt])
    P.memset("pool", FB[:, :], 0.0, [FB.tt])
    fcar = sb("fcar", [1, 2])
    P.memset("dve", fcar[:, :], 0.0, [fcar.tt])
    y01 = sb("y01", [128, TS], BF16)
    y23 = sb("y23", [128, TS], BF16)

    ebuf = [sb("ebuf%d" % i, [128, TS], BF16) for i in range(3)]
    fscr = sb("fscr", [128, TS])
    spb = [sb("spb%d" % i, [128, TS], BF16) for i in range(2)]
    gbuf = [sb("gbuf%d" % i, [128, TS], BF16) for i in range(2)]
    abuf = [sb("abuf%d" % i, [128, TS], BF16) for i in range(2)]
    pT = [sb("pT%d" % i, [128, TS], BF16) for i in range(2)]
    ebuf3 = ebuf
    rrow = fscr
    xsq = spb

    def sb64(name, cols=TS, dt=F32):
        return sb(name, [64, cols], dt)

    class ColView:
        def __init__(self, buf, off, n):
            self.buf, self.off, self.n, self.tt = buf, off, n, buf.tt

        def __getitem__(self, key):
            rs, cs = key
            st = 0 if cs.start is None else cs.start
            en = self.n if cs.stop is None else cs.stop
            return self.buf.t[rs, self.off + st:self.off + en]

    rbuf = sb64("rbuf", TS + 1)
    kbuf = sb64("kbuf", TS + 1)
    vbuf = sb64("vbuf", TS + 1)
    wabuf = sb64("wabuf", TS + 1)
    for b_ in (rbuf, kbuf, vbuf, wabuf):
        P.memset("pool", b_[:, 0:1], 0.0, [b_.tt])
    r_s = ColView(rbuf, 1, TS)
    k_s = ColView(kbuf, 1, TS)
    v_s = ColView(vbuf, 1, TS)
    wa_s = ColView(wabuf, 1, TS)
    tmpd = sb64("tmpd")
    wa_b = sb64("wa_b", TS, BF16)
    v_b = sb64("v_b", TS, BF16)
    t1 = sb64("t1")
    t2 = sb64("t2")
    ew = sb64("ew")
    cew = sb64("cew")
    alpha = sb64("alpha")
    kkn = sb64("kkn")
    kmod = sb64("kmod")
    kal = sb64("kal")
    e1 = sb64("e1")
    e2 = sb64("e2")
    e3 = tmpd
    e4 = sb64("e4")
    pc = sb("pc", [64, NCH])
    AR = sb("AR", [64, NCH, 2, CH], BF16)
    btl = sb64("btl", TS, BF16)
    ktl = sb64("ktl", TS, BF16)
    bhat = sb64("bhat", TS, BF16)
    khat = sb64("khat", TS, BF16)
    tok = sb("tok", [64, NCH, 4, CH], BF16)
    amat = sb("amat", [64, NCH, 320], BF16)
    XX = [sb("XX%d" % i, [64, NCH, 128], BF16) for i in range(2)]
    TTm = [sb("TTm%d" % i, [64, NCH, CH], BF16) for i in range(2)]
    Mst = sb("Mst", [64, 64])
    Mbf = sb("Mbf", [64, 64], BF16)
    P.memset("dve", Mst[:, :], 0.0, [Mst.tt])
    P.memset("dve", Mbf[:, :], 0.0, [Mbf.tt])
    Gbf = sb("Gbf", [64, 64], BF16)
    Ubf = sb("Ubf", [64, 64], BF16)
    yrw = k_s
    bonus = alpha
    rcp = e4

    lxb = sb64("lxb", TS + 3)
    P.memset("pool", lxb[:, 0:3], 0.0, [lxb.tt])
    xc = t1
    xcb = wa_b
    lr = t2
    li = e1
    la = e2
    lu = tmpd
    lh = e4
    lcar = sb("lcar", [64, 1])
    P.memset("dve", lcar[:, :], 0.0, [lcar.tt])

    vcol = lambda c, n=64: vec[0:n, c:c + 1]
    xcol = lambda c, n=64: vx[0:n, c:c + 1]

    def load_x(i):
        for k in range(8):
            P.dma(xt[:, k, :], xT_d[k * 128:(k + 1) * 128, i * TS:(i + 1) * TS], [], [xt.tt])

    def proj_fm(cg, M, kind="proj"):
        b = bank(kind)
        for k in range(8):
            P.mm(b[0:M, :], wbf[:, k, cg:cg + M], xn[:, k, :], k == 0, k == 7, [wbf.tt, xn.tt], [b.tt])
        return b

    def c3(ap):
        return ap.rearrange("p (c t) -> p c t", c=NCH)

    def sigmoid_exp(dst, dtt, src, stt_, nbias):
        rd = [stt_] + ([vx.tt] if not isinstance(nbias, float) else [])
        P.act(dst, src, AF.Exp, rd, [dtt], bias=nbias, scale=-1.0)
        P.act(dst, dst, AF.Ln, [dtt], [dtt], bias=1.0)
        P.act(dst, dst, AF.Exp, [dtt], [dtt], scale=-1.0)

    def shift(buf, mu_col, tmp):
        P.tt("pool", tmp[:, :], buf[:, 0:TS], buf[:, 1:TS + 1], ALU.subtract, [buf.tt], [tmp.tt])
        P.cp("pool", buf[:, 0:1], buf[:, TS:TS + 1], [buf.tt], [buf.tt])
        P.stt(buf[:, 1:TS + 1], tmp[:, :], vcol(mu_col), buf[:, 1:TS + 1], ALU.mult, ALU.add,
              [tmp.tt, vec.tt, buf.tt], [buf.tt])

    def rwkv_tile(i):
        shift(wabuf, V_MU_WA, tmpd)
        shift(kbuf, V_MU_K, e2)
        yield
        shift(rbuf, V_MU_R, e4)
        shift(vbuf, V_MU_V, t2)
        yield
        P.cp("dve", wa_b[:, :], wa_s[:, :], [wa_s.tt], [wa_b.tt])
        P.act(t1[0:32, :], wa_s[0:32, :], AF.Exp, [wa_s.tt], [t1.tt], scale=-2.0)
        P.act(t1[0:32, :], t1[0:32, :], AF.Ln, [t1.tt], [t1.tt], bias=1.0)
        P.act(t1[0:32, :], t1[0:32, :], AF.Exp, [t1.tt], [t1.tt], scale=-1.0)
        P.ts("dve", wa_b[0:32, :], t1[0:32, :], 2.0, ALU.mult, [t1.tt], [wa_b.tt], -1.0, ALU.add)
        P.cp("pool", v_b[:, :], v_s[:, :], [v_s.tt], [v_b.tt])
        yield
        bw = ps[1]
        P.mm(bw[0:64, :], wsmb[:, 0:64], wa_b[:, :], True, True, [wsmb.tt, wa_b.tt], [bw.tt])
        P.act(e1[:, :], bw[0:64, :], AF.Exp, [bw.tt, vx.tt], [e1.tt], bias=xcol(X_NW0), scale=-1.0)
        ba = ps[1]
        P.mm(ba[0:64, :], wsmb[:, 64:128], wa_b[:, :], True, True, [wsmb.tt, wa_b.tt], [ba.tt])
        sigmoid_exp(alpha[:, :], alpha.tt, ba[0:64, :], ba.tt, xcol(X_NA0))
        yield
        P.act(t1[:, :], e1[:, :], AF.Ln, [e1.tt], [t1.tt], bias=1.0)
        P.act(ew[:, :], t1[:, :], AF.Exp, [t1.tt], [ew.tt], bias=-0.5, scale=-1.0)
        P.scan(cew[:, :], cst[0:64, C_RST:C_RST + TS], ew[:, :], 0.0, ALU.mult, ALU.add,
               [cst.tt, ew.tt], [cew.tt])
        P.ts("dve", t2[:, :], k_s[:, :], vcol(V_KK), ALU.mult, [k_s.tt, vec.tt], [t2.tt])
        P.tt("pool", e2[:, :], t2[:, :], t2[:, :], ALU.mult, [t2.tt], [e2.tt])
        yield
        b = ps[1]
        P.mm(b[0:64, :], onesf[0:64, 0:64], e2[:, :], True, True, [cst.tt, e2.tt], [b.tt])
        P.ts("dve", e3[:, :], b[0:64, :], 1e-12, ALU.max, [b.tt], [e3.tt])
        yield
        P.act(e3[:, :], e3[:, :], AF.Ln, [e3.tt], [e3.tt])
        P.act(e3[:, :], e3[:, :], AF.Exp, [e3.tt], [e3.tt], scale=-0.5)
        P.tt("dve", kkn[:, :], t2[:, :], e3[:, :], ALU.mult, [t2.tt, e3.tt], [kkn.tt])
        P.ts("dve", e4[:, :], alpha[:, :], vcol(V_KA), ALU.mult, [alpha.tt, vec.tt, vx.tt], [e4.tt],
             xcol(X_1MKA), ALU.add)
        P.tt("dve", kmod[:, :], k_s[:, :], e4[:, :], ALU.mult, [k_s.tt, e4.tt], [kmod.tt])
        P.tt("pool", kal[:, :], kkn[:, :], alpha[:, :], ALU.mult, [kkn.tt, alpha.tt], [kal.tt])
        P.stt(t2[:, :], r_s[:, :], vcol(V_RK), kmod[:, :], ALU.mult, ALU.mult, [r_s.tt, vec.tt, kmod.tt], [t2.tt])
        yield
        b = ps[1]
        P.mm(b[0:64, :], onesf[0:64, 0:64], t2[:, :], True, True, [cst.tt, t2.tt], [b.tt])
        P.tt("dve", bonus[:, :], b[0:64, :], v_s[:, :], ALU.mult, [b.tt, v_s.tt], [bonus.tt])
        yield
        cew3 = c3(cew[:, :])
        cend = cew3[:, :, CH - 1:CH].to_broadcast([64, NCH, CH])
        P.act(e1[:, :], cew[:, :], AF.Exp, [cew.tt], [e1.tt], scale=-1.0)
        P.tt("dve", AR[:, :, 1, :], c3(r_s[:, :]), c3(e1[:, :]), ALU.mult, [r_s.tt, e1.tt], [AR.tt])
        P.act(e2[:, :], cew[:, :], AF.Exp, [cew.tt], [e2.tt])
        P.tt("dve", btl[:, :], kal[:, :], e2[:, :], ALU.mult, [kal.tt, e2.tt], [btl.tt])
        P.tt("pool", ktl[:, :], kmod[:, :], e2[:, :], ALU.mult, [kmod.tt, e2.tt], [ktl.tt])
        yield
        P.tt("pool", e3[:, :], cew[:, :], ew[:, :], ALU.subtract, [cew.tt, ew.tt], [e3.tt])
        P.act(e3[:, :], e3[:, :], AF.Exp, [e3.tt], [e3.tt], scale=-1.0)
        P.stt(AR[:, :, 0, :], c3(kkn[:, :]), -1.0, c3(e3[:, :]), ALU.mult, ALU.mult, [kkn.tt, e3.tt], [AR.tt])
        yield
        P.tt("dve", c3(e4[:, :]), cend, cew3, ALU.subtract, [cew.tt], [e4.tt])
        P.act(e4[:, :], e4[:, :], AF.Exp, [e4.tt], [e4.tt], scale=-1.0)
        P.tt("dve", bhat[:, :], kal[:, :], e4[:, :], ALU.mult, [kal.tt, e4.tt], [bhat.tt])
        P.tt("pool", khat[:, :], kmod[:, :], e4[:, :], ALU.mult, [kmod.tt, e4.tt], [khat.tt])
        P.act(pc[:, :], cew3[:, :, CH - 1], AF.Exp, [cew.tt], [pc.tt], scale=-1.0)

    def rwkv_stream(i, flags=None):
        B_RW = ps[1]
        id64 = cbf[0:64, C_ID:C_ID + 64]
        for hh in range(2):
            for cl in range(4):
                c = hh * 4 + cl
                cs = slice(c * CH, (c + 1) * CH)
                for a_, src in enumerate((bhat, khat, v_b)):
                    o = (cl * 4 + a_) * CH
                    P.tr(psT[0:64, o:o + CH], src[:, cs], id64, [src.tt, cbf.tt], [psT.tt])
                o = (cl * 4 + 3) * CH
                P.tr(psT[0:64, o:o + CH], AR[:, c, 0, :], id64, [AR.tt, cbf.tt], [psT.tt])
            P.cp("act", tok[:, hh * 4:hh * 4 + 4, :, :].rearrange("p c a t -> p (c a t)"), psT[0:64, 0:1024],
                 [psT.tt], [tok.tt])
            yield
        for c in range(NCH):
            cs = slice(c * CH, (c + 1) * CH)
            b = B_RW
            arc = AR[:, c, :, :].rearrange("p a t -> p (a t)")
            P.mm(b[0:64, 0:128], btl[:, cs], arc, True, True, [btl.tt, AR.tt], [b.tt])
            P.mm(b[0:64, 128:256], ktl[:, cs], arc, True, True, [ktl.tt, AR.tt], [b.tt])
            P.mm(b[0:64, 256:320], AR[:, c, 0, :], btl[:, cs], True, True, [btl.tt, AR.tt], [b.tt])
            P.tt("dve", amat[:, c, :], b[0:64, 0:320], cst[0:64, C_M320:C_M320 + 320], ALU.mult,
                 [b.tt, cst.tt], [amat.tt])
            yield
        P.tt("dve", TTm[0][:, :, :], amat[:, :, 0:64], id64.unsqueeze(1).to_broadcast([64, NCH, CH]), ALU.add,
             [amat.tt, cbf.tt], [TTm[0].tt])
        for lv in range(1, 6):
            dst = XX[lv % 2]
            for hh in range(2):
                b = B_RW
                for cl in range(4):
                    c = hh * 4 + cl
                    if lv == 1:
                        X_, XT_, rd = amat[:, c, 256:320], amat[:, c, 0:64], amat.tt
                    else:
                        X_, XT_, rd = XX[(lv - 1) % 2][:, c, 0:64], XX[(lv - 1) % 2][:, c, 64:128], XX[(lv - 1) % 2].tt
                    P.mm(b[0:64, cl * 128:cl * 128 + 64], XT_, X_, True, True, [rd], [b.tt])
                    P.mm(b[0:64, cl * 128 + 64:cl * 128 + 128], X_, XT_, True, True, [rd], [b.tt])
                P.cp("act", dst[:, hh * 4:hh * 4 + 4, :].rearrange("p c t -> p (c t)"), b[0:64, :], [b.tt], [dst.tt])
                yield
            b = B_RW
            told, tnew = TTm[(lv - 1) % 2], TTm[lv % 2]
            for c in range(NCH):
                P.mm(b[0:64, c * CH:(c + 1) * CH], dst[:, c, 0:64], told[:, c, :], True, True,
                     [dst.tt, told.tt], [b.tt])
            P.tt("dve", tnew[:, :, :], c3(b[0:64, :]), told[:, :, :], ALU.add, [b.tt, told.tt], [tnew.tt])
            yield
        Tf = TTm[5 % 2]
        xx0 = XX[0][:, :, :].rearrange("p c t -> p (c t)")
        xx1 = XX[1][:, :, :].rearrange("p c t -> p (c t)")
        G2bf, W1T, U2 = xx0[:, 0:TS], xx0[:, TS:2 * TS], xx1[:, 0:TS]
        b = B_RW
        for c in range(NCH):
            P.mm(b[0:64, c * CH:(c + 1) * CH], amat[:, c, 128:192], tok[:, c, 2, :], True, True,
                 [amat.tt, tok.tt], [b.tt])
        P.cp("act", G2bf, b[0:64, :], [b.tt, Tf.tt], [XX[0].tt])
        yield
        for c in range(NCH):
            P.mm(b[0:64, c * CH:(c + 1) * CH], tok[:, c, 3, :], Tf[:, c, :], True, True, [tok.tt, Tf.tt], [b.tt])
        P.cp("act", W1T, b[0:64, :], [b.tt], [XX[0].tt])
        yield
        for c in range(NCH):
            P.mm(b[0:64, c * CH:(c + 1) * CH], Tf[:, c, :], G2bf[:, c * CH:(c + 1) * CH], True, True,
                 [Tf.tt, XX[0].tt], [b.tt])
        P.cp("act", U2, b[0:64, :], [b.tt, XX[1].tt], [XX[1].tt])
        yield
        PS_C = B_RW
        for c in range(NCH):
            cs = slice(c * CH, (c + 1) * CH)
            vt = tok[:, c, 2, :]
            P.mm(PS_C[0:64, 64:128], W1T[:, cs], Mbf[:, :], True, True, [XX[0].tt, Mbf.tt], [PS_C.tt])
            P.tt("dve", Ubf[:, :], PS_C[0:64, 64:128], U2[:, cs], ALU.add, [PS_C.tt, XX[1].tt], [Ubf.tt])
            yield
            P.mm(PS_C[0:64, 128:192], tok[:, c, 0, :], Ubf[:, :], True, False, [tok.tt, Ubf.tt], [PS_C.tt])
            P.mm(PS_C[0:64, 128:192], tok[:, c, 1, :], vt, False, True, [tok.tt], [PS_C.tt])
            P.mm(PS_C[0:64, 192:256], Mbf[:, :], AR[:, c, 1, :], True, False, [Mbf.tt, AR.tt], [PS_C.tt])
            P.mm(PS_C[0:64, 192:256], Ubf[:, :], amat[:, c, 64:128], False, False, [Ubf.tt, amat.tt], [PS_C.tt])
            P.mm(PS_C[0:64, 192:256], vt, amat[:, c, 192:256], False, True, [tok.tt, amat.tt], [PS_C.tt])
            P.stt(Mst[:, :], Mst[:, :], pc[:, c:c + 1], PS_C[0:64, 128:192], ALU.mult, ALU.add,
                  [Mst.tt, pc.tt, PS_C.tt], [Mst.tt])
            P.cp("dve", Mbf[:, :], Mst[:, :], [Mst.tt], [Mbf.tt])
            P.cp("act", yrw[:, cs], PS_C[0:64, 192:256], [PS_C.tt], [yrw.tt])
            yield
        while flags is not None and not flags["lru"]:
            yield
        b = B_RW
        P.mm(b[0:64, :], onesf[0:64, 0:64], yrw[:, :], True, True, [cst.tt, yrw.tt], [b.tt])
        P.stt(t1[:, :], b[0:64, :], -1.0 / 64, yrw[:, :], ALU.mult, ALU.add, [b.tt, yrw.tt], [t1.tt])
        P.tt("pool", t2[:, :], t1[:, :], t1[:, :], ALU.mult, [t1.tt], [t2.tt])
        yield
        P.mm(b[0:64, :], onesf[0:64, 0:64], t2[:, :], True, True, [cst.tt, t2.tt], [b.tt])
        P.act(e1[:, :], b[0:64, :], AF.Ln, [b.tt], [e1.tt], bias=GN_EPS, scale=1.0 / 64)
        P.act(e1[:, :], e1[:, :], AF.Exp, [e1.tt], [e1.tt], scale=-0.5)
        P.tt("dve", t1[:, :], t1[:, :], e1[:, :], ALU.mult, [t1.tt, e1.tt], [t1.tt])
        P.ts("dve", t1[:, :], t1[:, :], vcol(V_LNG), ALU.mult, [t1.tt, vec.tt], [t1.tt], vcol(V_LNB), ALU.add)
        P.tt("dve", t1[:, :], t1[:, :], bonus[:, :], ALU.add, [t1.tt, bonus.tt], [t1.tt])
        P.tt("dve", y23[0:64, :], t1[:, :], sgr[:, :], ALU.mult, [t1.tt, sgr.tt], [y23.tt])

    def lru_pre(i, lbank=1):
        P.ts("dve", xc[:, :], lxb[:, 0:TS], vcol(V_CW), ALU.mult, [lxb.tt, vec.tt], [xc.tt], vcol(V_CB), ALU.add)
        for j in range(1, 4):
            P.stt(xc[:, :], lxb[:, j:j + TS], vcol(V_CW + j), xc[:, :], ALU.mult, ALU.add,
                  [lxb.tt, vec.tt, xc.tt], [xc.tt])
        P.cp("pool", lxb[:, 0:3], lxb[:, TS:TS + 3], [lxb.tt], [lxb.tt])
        P.cp("dve", xcb[:, :], xc[:, :], [xc.tt], [xcb.tt])
        yield
        b = ps[lbank]
        P.mm(b[0:64, :], wsmb[:, 128:192], xcb[:, :], True, True, [wsmb.tt, xcb.tt], [b.tt])
        sigmoid_exp(lr[:, :], lr.tt, b[0:64, :], b.tt, xcol(X_NBA))
        yield
        P.mm(b[0:64, :], wsmb[:, 192:256], xcb[:, :], True, True, [wsmb.tt, xcb.tt], [b.tt])
        sigmoid_exp(li[:, :], li.tt, b[0:64, :], b.tt, xcol(X_NBX))
        yield

    def lru_stream(i):
        P.act(la[:, :], lr[:, :], AF.Exp, [lr.tt, vx.tt], [la.tt], scale=xcol(X_CL))
        P.act(lu[:, :], lr[:, :], AF.Exp, [lr.tt, vx.tt], [lu.tt], scale=xcol(X_2CL))
        P.ts("dve", lu[:, :], lu[:, :], -1.0, ALU.mult, [lu.tt], [lu.tt], 1.0, ALU.add)
        P.ts("dve", lu[:, :], lu[:, :], 1e-18, ALU.max, [lu.tt], [lu.tt])
        yield
        P.act(lu[:, :], lu[:, :], AF.Ln, [lu.tt], [lu.tt])
        P.act(lu[:, :], lu[:, :], AF.Exp, [lu.tt], [lu.tt], scale=0.5)
        P.tt("dve", lu[:, :], lu[:, :], li[:, :], ALU.mult, [lu.tt, li.tt], [lu.tt])
        P.tt("dve", lu[:, :], lu[:, :], xc[:, :], ALU.mult, [lu.tt, xc.tt], [lu.tt])
        yield
        P.scan(lh[:, :], la[:, :], lu[:, :], lcar[:, 0:1], ALU.mult, ALU.add, [la.tt, lu.tt, lcar.tt], [lh.tt])
        P.cp("dve", lcar[:, 0:1], lh[:, TS - 1:TS], [lh.tt], [lcar.tt])
        P.tt("dve", y23[64:128, :], lh[:, :], sgl[:, :], ALU.mult, [lh.tt, sgl.tt], [y23.tt])

    def attn_stream(i):
        nkb = 4 * (i + 1)
        m128 = cbf[:, C_M128:C_M128 + 128]
        B_FS = (ps[2], ps[3])
        F_O = ps[4]
        B_SS, PS_R, S_O = ps[0], ps[5], ps[6]
        order = list(range(nkb - 1, -1, -1))

        def fgeom(j):
            d = j - 4 * i
            return d, (0 if d < 0 else 128 * d)

        def fox_scores(j):
            d, col0 = fgeom(j)
            s_ = B_FS[j % 2]
            kc = slice(j * 128, (j + 1) * 128)
            P.mm(s_[:, col0:TS], kaug[:, kc], qaug[:, col0:TS], True, d < 0, [kaug_tt[j // 4], qaug.tt], [s_.tt])
            if d >= 0:
                P.mm(s_[:, col0:col0 + 128], ident, cbf[:, C_FOXM:C_FOXM + 128], False, True, [cbf.tt], [s_.tt])

        def fox_exp(j):
            d, col0 = fgeom(j)
            P.act(pT[j % 2][:, col0:TS], B_FS[j % 2][:, col0:TS], AF.Exp, [B_FS[j % 2].tt], [pT[j % 2].tt])

        def fox_pv(j):
            d, col0 = fgeom(j)
            pt = pT[j % 2]
            P.mm(F_O[0:65, col0:TS], vfox[:, j, :], pt[:, col0:TS], j == 0, j == nkb - 1,
                 [vfox_tt[j // 4], pt.tt], [F_O.tt])

        def geom(t):
            j = order[t]
            d = j - 4 * i
            col0 = 0 if d < 0 else 128 * d
            cr = 0 if t == 0 else col0
            return j, d, col0, cr

        def sb_s(t):
            j, d, col0, cr = geom(t)
            kc = slice(j * 128, (j + 1) * 128)
            P.mm(B_SS[:, col0:TS], ksb[:, kc], qsb[:, col0:TS], True, True, [ksb_tt[j // 4], qsb.tt], [B_SS.tt])

        def sb_e(t):
            j, d, col0, cr = geom(t)
            e_, sp_ = ebuf3[t % 3], spb[t % 2]
            P.act(e_[:, col0:TS], B_SS[:, col0:TS], AF.Exp, [B_SS.tt], [e_.tt])
            P.act(sp_[:, col0:TS], e_[:, col0:TS], AF.Ln, [e_.tt], [sp_.tt], bias=1.0)
            if d >= 0:
                P.tt("pool", sp_[:, col0:col0 + 128], sp_[:, col0:col0 + 128], m128, ALU.mult, [sp_.tt, cbf.tt], [sp_.tt])
            if t == 0 and col0 > 0:
                P.memset("pool", sp_[:, 0:col0], 0.0, [sp_.tt])

        def sb_ru(t):
            j, d, col0, cr = geom(t)
            sp_ = spb[t % 2]
            P.mm(PS_R[:, cr:TS], cbf[:, C_NUINC:C_NUINC + 128], sp_[:, cr:TS], t == 0, False, [cbf.tt, sp_.tt], [PS_R.tt])

        def sb_g(t):
            j, d, col0, cr = geom(t)
            g_ = gbuf[t % 2]
            P.act(g_[:, col0:TS], PS_R[:, col0:TS], AF.Exp, [PS_R.tt], [g_.tt])

        def sb_rl(t):
            j, d, col0, cr = geom(t)
            sp_ = spb[t % 2]
            P.mm(PS_R[:, cr:TS], cbf[:, C_NLOW:C_NLOW + 128], sp_[:, cr:TS], False, t == nkb - 1, [cbf.tt, sp_.tt], [PS_R.tt])

        def sb_a(t):
            j, d, col0, cr = geom(t)
            e_, g_, a_ = ebuf3[t % 3], gbuf[t % 2], abuf[t % 2]
            if t == 0 and col0 > 0:
                P.memset("pool", a_[:, 0:col0], 0.0, [a_.tt])
            P.tt("pool", a_[:, col0:TS], e_[:, col0:TS], g_[:, col0:TS], ALU.mult, [e_.tt, g_.tt], [a_.tt])
            if d >= 0:
                P.tt("pool", a_[:, col0:col0 + 128], a_[:, col0:col0 + 128], m128, ALU.mult, [a_.tt, cbf.tt], [a_.tt])

        def sb_pv(t):
            j, d, col0, cr = geom(t)
            a_ = abuf[t % 2]
            P.mm(S_O[0:64, cr:TS], vsb[:, j, 0:64], a_[:, cr:TS], t == 0, t == nkb - 1, [vsb_tt[j // 4], a_.tt], [S_O.tt])

        fox_scores(0)
        for t in range(nkb + 2):
            if 0 <= t - 1 < nkb:
                sb_ru(t - 1)
                sb_g(t - 1)
            if t < nkb:
                sb_s(t)
            if t + 1 < nkb:
                fox_scores(t + 1)
            if t < nkb:
                fox_exp(t)
                sb_e(t)
            if 0 <= t - 1 < nkb:
                sb_rl(t - 1)
            if 0 <= t - 2 < nkb:
                sb_a(t - 2)
                sb_pv(t - 2)
            if t < nkb:
                fox_pv(t)
            yield
        P.tt("dve", y01[64:128, :], S_O[0:64, :], sgs[:, :], ALU.mult, [S_O.tt, sgs.tt], [y01.tt])
        P.recip(rrow[64:65, :], F_O[64:65, :], [F_O.tt], [rrow.tt])
        bb = B_FS[0]
        P.mm(bb[0:64, :], onesf[64:65, 0:64], rrow[64:65, :], True, True, [cst.tt, rrow.tt], [bb.tt])
        P.cp("act", rcp[:, :], bb[0:64, :], [bb.tt], [rcp.tt])
        P.tt("dve", rcp[:, :], F_O[0:64, :], rcp[:, :], ALU.mult, [F_O.tt, rcp.tt], [rcp.tt])
        P.tt("dve", y01[0:64, :], rcp[:, :], sgf[:, :], ALU.mult, [rcp.tt, sgf.tt], [y01.tt])

    def run_streams(streams):
        alive = [[g, w] for g, w in streams]
        while alive:
            for ent in list(alive):
                for _ in range(ent[1]):
                    try:
                        next(ent[0])
                    except StopIteration:
                        alive.remove(ent)
                        break

    def reset_state():
        P.memset("dve", fcar[:, :], 0.0, [fcar.tt])
        for b_ in (rbuf, kbuf, vbuf, wabuf):
            P.memset("pool", b_[:, 0:1], 0.0, [b_.tt])
        P.memset("dve", Mst[:, :], 0.0, [Mst.tt])
        P.memset("dve", Mbf[:, :], 0.0, [Mbf.tt])
        P.memset("pool", lxb[:, 0:3], 0.0, [lxb.tt])
        P.memset("dve", lcar[:, :], 0.0, [lcar.tt])

    def front_a1(l, i):
        xsq_, rstd_ = (FB, FB2), FS
        if l > 0:
            P.dma(ypb[:, :, :], yg_tile(l - 1, i), [yg_tt[l - 1][i // CQ]], [ypb.tt])
            for m in range(8):
                b = ps[1]
                for k in range(8):
                    P.mm(b[:, :], woutb[:, k, m * 128:(m + 1) * 128], ypb[:, k, :], k == 0, k == 7,
                         [woutb.tt, ypb.tt], [b.tt])
                P.tt("dve", xt[:, m, :], xt[:, m, :], b[:, :], ALU.add, [xt.tt, b.tt], [xt.tt])
                yield
        b = ps[1]
        for k in range(8):
            xq = xsq_[k % 2]
            P.tt("pool", xq[:, :], xt[:, k, :], xt[:, k, :], ALU.mult, [xt.tt], [xq.tt])
            P.mm(b[:, :], onesb, xq[:, :], k == 0, k == 7, [cbf.tt, xq.tt], [b.tt])
        P.act(rstd_[:, :], b[:, :], AF.Ln, [b.tt], [rstd_.tt], bias=RMS_EPS, scale=1.0 / D)
        P.act(rstd_[:, :], rstd_[:, :], AF.Exp, [rstd_.tt], [rstd_.tt], scale=-0.5)
        yield
        for k in range(8):
            P.tt("dve" if k % 2 == 0 else "pool", xn[:, k, :], xt[:, k, :], rstd_[:, :], ALU.mult,
                 [xt.tt, rstd_.tt], [xn.tt])
            if k % 4 == 3:
                yield
        if i + 1 < NT:
            load_x(i + 1)

    def front_a2(l, i):
        c0 = i * TS
        b = proj_fm(CG_QQ, 128, "pa")
        P.ts("dve", qaug[0:64, :], b[0:64, :], 0.125, ALU.mult, [b.tt], [qaug.tt])
        P.ts("dve", qsb[0:64, :], b[64:128, :], 0.125, ALU.mult, [b.tt], [qsb.tt])
        yield
        b = proj_fm(CG_KK, 128, "pa")
        P.cp("dve", kaug[0:64, c0:c0 + TS], b[0:64, :], [b.tt], [kaug_tt[i]])
        P.cp("act", ksb[0:64, c0:c0 + TS], b[64:128, :], [b.tt], [ksb_tt[i]])
        yield
        b = proj_fm(CG_GA, 128, "pa")
        sigmoid_exp(FS[:, :], FS.tt, b[:, :], b.tt, 0.0)
        P.tt("dve", sgf[:, :], b[0:64, :], FS[0:64, :], ALU.mult, [b.tt, FS.tt], [sgf.tt])
        P.tt("dve", sgs[:, :], b[64:128, :], FS[64:128, :], ALU.mult, [b.tt, FS.tt], [sgs.tt])
        yield
        b = proj_fm(CG_FF, 1, "pa")
        P.act(FS[0:1, :], b[0:1, :], AF.Exp, [b.tt, vx.tt], [FS.tt], bias=xcol(X_NBF, 1), scale=-1.0)
        P.act(FS[0:1, :], FS[0:1, :], AF.Ln, [FS.tt], [FS.tt], bias=1.0)
        P.scan(FS[32:33, :], cst[0:1, C_ONE5:C_ONE5 + TS], FS[0:1, :], fcar[:, 0:1], ALU.mult, ALU.add,
               [cst.tt, FS.tt, fcar.tt], [FS.tt])
        P.cp("dve", fcar[:, 0:1], FS[32:33, TS - 1:TS], [FS.tt], [fcar.tt])
        P.cp("dve", FB[32:33, :], FS[32:33, :], [FS.tt], [FB.tt])
        P.tt("dve", FS[64:65, :], FS[32:33, :], FB[32:33, :], ALU.subtract, [FS.tt, FB.tt], [FS.tt])
        P.cp("dve", FB[64:65, :], FS[64:65, :], [FS.tt], [FB.tt])
        P.tt("dve", FS[0:1, :], FS[64:65, :], FB[64:65, :], ALU.subtract, [FS.tt, FB.tt], [FS.tt])
        P.cp("dve", FB[0:1, :], FS[0:1, :], [FS.tt], [FB.tt])
        P.ts("dve", FB2[:, :], FB[:, :], -1.0, ALU.mult, [FB.tt], [FB2.tt])
        for r3, p_ in enumerate((32, 64, 0)):
            P.dma(kaug[67 + r3:68 + r3, c0:c0 + TS], FB[p_:p_ + 1, :], [FB.tt], [kaug_tt[i]])
            P.dma(qaug[64 + r3:65 + r3, :], FB2[p_:p_ + 1, :], [FB2.tt], [qaug.tt])
        yield
        b = bank("pa")
        for s4 in range(4):
            for k in range(8):
                P.mm(b[:, s4 * 128:(s4 + 1) * 128], xn[:, k, s4 * 128:(s4 + 1) * 128],
                     wbf[:, k, CG_VV:CG_VV + 128], k == 0, k == 7, [wbf.tt, xn.tt], [b.tt])
        bv = b[:, :].rearrange("p (s c) -> p s c", s=4)
        P.cp("dve", vfox[:, 4 * i:4 * i + 4, 0:64], bv[:, :, 0:64], [b.tt], [vfox_tt[i]])
        P.cp("act", vsb[:, 4 * i:4 * i + 4, 0:64], bv[:, :, 64:128], [b.tt], [vsb_tt[i]])

    def front_b(l, i):
        b = proj_fm(CG_GB, 128)
        sigmoid_exp(FS[:, :], FS.tt, b[:, :], b.tt, 0.0)
        P.tt("dve", sgr[:, :], b[0:64, :], FS[0:64, :], ALU.mult, [b.tt, FS.tt], [sgr.tt])
        P.tt("dve", sgl[:, :], b[64:128, :], FS[64:128, :], ALU.mult, [b.tt, FS.tt], [sgl.tt])
        b = proj_fm(CG_RK, 128)
        P.cp("dve", rbuf[:, 1:TS + 1], b[0:64, :], [b.tt], [rbuf.tt])
        P.cp("act", kbuf[:, 1:TS + 1], b[64:128, :], [b.tt], [kbuf.tt])
        b = proj_fm(CG_VWA, 128)
        P.cp("dve", vbuf[:, 1:TS + 1], b[0:64, :], [b.tt], [vbuf.tt])
        P.cp("act", wabuf[:, 1:TS + 1], b[64:128, :], [b.tt], [wabuf.tt])
        b = proj_fm(CG_LX, 64)
        P.cp("dve", lxb[:, 3:TS + 3], b[0:64, :], [b.tt], [lxb.tt])

    def run_layer(l):
        load_small(l)
        if l == 0:
            for _ in load_win(0):
                pass
        reset_state()
        load_x(0)
        for _ in front_a1(l, 0):
            pass
        for _ in front_a2(l, 0):
            pass
        for i in range(NT):
            c0 = i * TS
            front_b(l, i)

            flags = {"pre": False, "lru": False}
            lru_in_attn = (4 * (i + 1) + 2) < 40

            def attn_all(i=i, flags=flags, lru_in_attn=lru_in_attn):
                yield from attn_stream(i)
                if lru_in_attn:
                    while not flags["pre"]:
                        yield
                    yield from lru_pre(i, 0)
                    yield from lru_stream(i)
                    flags["lru"] = True
                if i + 1 < NT:
                    while not flags.get("fa1"):
                        yield
                    yield from front_a2(l, i + 1)

            def rw_all(i=i, flags=flags, lru_in_attn=lru_in_attn):
                yield from rwkv_tile(i)
                flags["pre"] = True
                if not lru_in_attn:
                    yield from lru_pre(i)
                    yield from lru_stream(i)
                    flags["lru"] = True
                yield from rwkv_stream(i, flags)
            def fa1_all(i=i, flags=flags):
                yield from front_a1(l, i + 1)
                flags["fa1"] = True
            strs = [(attn_all(), 1), (rw_all(), 1)]
            if i + 1 < NT:
                strs.append((fa1_all(), 1))
            else:
                strs.append((prefetch_next(l), 1))
            run_streams(strs)
            q_, o_ = i // CQ, (i % CQ) * TS
            P.dma(yown_ds[l][q_][0:128, o_:o_ + TS], y01[:, :], [y01.tt], [yown_tt[l][i][0]])
            P.dma(yown_ds[l][q_][128:256, o_:o_ + TS], y23[:, :], [y23.tt], [yown_tt[l][i][1]])
            if i % CQ == CQ - 1:
                rd = [t for ii in range(q_ * CQ, (q_ + 1) * CQ) for t in yown_tt[l][ii]]
                grp = [list(g) for g in groups]
                P.op("pool", (lambda l_, q2: (lambda e: e.collective_compute(
                    "AllGather", ALU.bypass, replica_groups=grp, ins=[yown_ds[l_][q2]], outs=[yg_ds[l_][q2]])))(l, q_),
                    rd, [yg_tt[l][q_]], cc=True)

    def run_final():
        wf = [woutb, wbf]
        yb = [ypb, xn]
        x2_tt = [TT("x2_%d" % i) for i in range(NT)]
        ssq_tt = [TT("ssq_%d" % i) for i in range(NT)]
        ssqr_tt = TT("ssqr")
        for i in range(NT):
            c0 = i * TS
            for l in range(nlayers):
                P.dma(yb[l][:, :, :], yg_tile(l, i), [yg_tt[l][i // CQ]], [yb[l].tt])
            P.dma(xt[:, 0:2, :], xfin_d[:, c0:c0 + TS].rearrange("(j p) t -> p j t", p=128), [], [xt.tt])
            bq = bank("sc")
            for j in range(2):
                b = bank("proj")
                n = 0
                for l in range(nlayers):
                    for k in range(8):
                        P.mm(b[:, :], wf[l][:, k, j * 128:(j + 1) * 128], yb[l][:, k, :], n == 0,
                             n == 8 * nlayers - 1, [wf[l].tt, yb[l].tt], [b.tt])
                        n += 1
                P.tt("dve", xt[:, j, :], xt[:, j, :], b[:, :], ALU.add, [xt.tt, b.tt], [xt.tt])
                xq = xsq[j]
                P.tt("pool", xq[:, :], xt[:, j, :], xt[:, j, :], ALU.mult, [xt.tt], [xq.tt])
                P.mm(bq[0:1, :], onesb[:, 0:1], xq[:, :], j == 0, j == 1, [cbf.tt, xq.tt], [bq.tt])
            P.cp("dve", t1[0:1, :], bq[0:1, :], [bq.tt], [t1.tt])
            P.dma(ssq_d[0:1, c0:c0 + TS], t1[0:1, :], [t1.tt], [ssq_tt[i]])
            P.dma(x2_d[:, c0:c0 + TS].rearrange("(j p) t -> p j t", p=128), xt[:, 0:2, :], [xt.tt], [x2_tt[i]])
        grp = [list(g) for g in groups]
        P.op("pool", lambda e: e.collective_compute("AllReduce", ALU.add, replica_groups=grp,
                                                    ins=[ssq_d], outs=[ssqr_d]),
             ssq_tt, [ssqr_tt], cc=True)
        for i in range(NT):
            c0 = i * TS
            sl = 2 * (i % 4)
            P.dma(xt[:, sl:sl + 2, :], x2_d[:, c0:c0 + TS].rearrange("(j p) t -> p j t", p=128), [x2_tt[i]], [xt.tt])
            rs_ = fscr
            P.dma(rs_[:, :], ssqr_d[0:1, c0:c0 + TS].partition_broadcast(128), [ssqr_tt], [rs_.tt])
            P.act(rs_[:, :], rs_[:, :], AF.Ln, [rs_.tt], [rs_.tt], bias=RMS_EPS, scale=1.0 / D)
            P.act(rs_[:, :], rs_[:, :], AF.Exp, [rs_.tt], [rs_.tt], scale=-0.5)
            for j in range(2):
                P.stt(xt[:, sl + j, :], xt[:, sl + j, :], vec[:, V_FG + j:V_FG + j + 1], rs_[:, :], ALU.mult, ALU.mult,
                      [xt.tt, vec.tt, rs_.tt], [xt.tt])
            P.dma(out_d[:, c0:c0 + TS].rearrange("(j p) t -> p j t", p=128), xt[:, sl:sl + 2, :], [xt.tt], [])

    for l in range(nlayers):
        run_layer(l)
    run_final()
    P.finish()
    P.emit()
    es.close()
    P.sb_used = sb_used[0]
    return nc, P


def perm_wout(w):
    return np.ascontiguousarray(w.reshape(4, 4, 64, D).transpose(1, 0, 2, 3).reshape(D, D))


def make_maps(inp, B, nlayers=DEPTH):
    consts = make_consts()
    x = inp["x"]
    xT = [np.ascontiguousarray(x[b].T) for b in range(B)]
    wp = [perm_wout(inp["w_out"][l]) for l in range(nlayers)]
    maps = []
    for c in range(4 * B):
        b, h = divmod(c, 4)
        m = {"xT": xT[b], "consts": consts, "xfin": np.ascontiguousarray(xT[b][256 * h:256 * h + 256, :])}
        for l in range(nlayers):
            p = pack_layer_inputs(inp, l, h)
            m["win%d" % l] = p["win"]
            m["vecs%d" % l] = p["vecs"]
            m["wsm%d" % l] = p["wsm"]
            m["woutp%d" % l] = wp[l]
            m["woutf%d" % l] = np.ascontiguousarray(wp[l][:, 256 * h:256 * h + 256])
        maps.append(m)
    return maps


def kernel(**inputs):
    inp = {k: np.asarray(v, dtype=np.float32) for k, v in inputs.items()}
    B, S, _ = inp["x"].shape
    nc, _ = build_fused(S)
    maps = make_maps(inp, B)
    res = run_bass_kernel_spmd(nc, maps, core_ids=list(range(4 * B)))
    out = np.empty((B, S, D), np.float32)
    for c in range(4 * B):
        b, h = divmod(c, 4)
        out[b, :, 256 * h:256 * h + 256] = res.results[c]["outT"].T
    return out
```
